# Optimizing a Trainium2 kernel written in Bass

```python
import jax, jax.numpy as jnp
from jax import lax
import numpy as np

D_MODEL = 1024
BATCH = 2
SEQ = 8192
DEPTH = 2

GRID_W = 64
CTX_LEN = 256
N_MIXERS = 2
BLK = 128
WINDOW = 128
ROPE_BASE = 10000.0
EPS = 1e-6
NEG_INF = -1e30

A_HEADS = 16
A_KV_HEADS = 4
A_GROUP = A_HEADS // A_KV_HEADS
A_HEAD_DIM = 64
A_WIDTH = A_HEADS * A_HEAD_DIM
A_KV_WIDTH = A_KV_HEADS * A_HEAD_DIM
A_IN = A_WIDTH + 2 * A_KV_WIDTH + A_WIDTH

B_HEADS = 16
B_NOPE = 64
B_ROPE = 32
B_V = 64
B_Q_RANK = 256
B_KV_RANK = 128
B_WIDTH = B_HEADS * B_V
B_IN = B_Q_RANK + B_KV_RANK + B_ROPE + B_WIDTH

kernel_name = "hybrid_swa_sink_mla_dit_prefix"


def rmsnorm(x, g):
    xf = x.astype(jnp.float32)
    y = xf * lax.rsqrt(jnp.mean(xf * xf, axis=-1, keepdims=True) + EPS)
    return (y * g.astype(jnp.float32)).astype(x.dtype)


def axial_rope_tables(n_tokens, rot_dim, dtype):
    n_rows = n_tokens // GRID_W
    row = jnp.broadcast_to(jnp.arange(n_rows)[:, None], (n_rows, GRID_W)).reshape(-1)
    col = jnp.broadcast_to(jnp.arange(GRID_W)[None, :], (n_rows, GRID_W)).reshape(-1)
    nf = rot_dim // 4
    inv = ROPE_BASE ** (-jnp.arange(nf, dtype=jnp.float32) / nf)
    ar = row.astype(jnp.float32)[:, None] * inv
    ac = col.astype(jnp.float32)[:, None] * inv
    ang = jnp.concatenate([ar, ar, ac, ac], axis=-1)
    return jnp.cos(ang).astype(dtype), jnp.sin(ang).astype(dtype)


def apply_rope(x, cos, sin):
    a1, a2, b1, b2 = jnp.split(x, 4, axis=-1)
    rot = jnp.concatenate([-a2, a1, -b2, b1], axis=-1)
    shp = (x.shape[1],) + (1,) * (x.ndim - 3) + (x.shape[-1],)
    return x * cos.reshape(shp) + rot * sin.reshape(shp)


def modulate(x, g, shift, scale):
    return rmsnorm(x, g) * (1 + scale) + shift


def mixer_a(h, hc, w_in, sinks, w_o, cos, sin, ctx_out):
    bsz, S, _ = h.shape
    dt = h.dtype
    q, k, v, g = jnp.split(h @ w_in, [A_WIDTH, A_WIDTH + A_KV_WIDTH, A_WIDTH + 2 * A_KV_WIDTH], axis=-1)
    q = apply_rope(q.reshape(bsz, S, A_KV_HEADS, A_GROUP, A_HEAD_DIM), cos, sin)
    k = apply_rope(k.reshape(bsz, S, A_KV_HEADS, A_HEAD_DIM), cos, sin)
    v = v.reshape(bsz, S, A_KV_HEADS, A_HEAD_DIM)
    kvc = hc @ w_in[:, A_WIDTH:A_WIDTH + 2 * A_KV_WIDTH]
    kc, vc = jnp.split(kvc.reshape(bsz, -1, 2 * A_KV_HEADS, A_HEAD_DIM), 2, axis=2)
    sink_logit = sinks.astype(jnp.float32).reshape(A_KV_HEADS, A_GROUP)
    scale = A_HEAD_DIM ** -0.5
    kp = jnp.pad(k, ((0, 0), (BLK, BLK), (0, 0), (0, 0)))
    vp = jnp.pad(v, ((0, 0), (BLK, BLK), (0, 0), (0, 0)))

    def block(i):
        qs = lax.dynamic_slice_in_dim(q, i * BLK, BLK, axis=1)
        ks = lax.dynamic_slice_in_dim(kp, i * BLK, 3 * BLK, axis=1)
        vs = lax.dynamic_slice_in_dim(vp, i * BLK, 3 * BLK, axis=1)
        q_pos = i * BLK + jnp.arange(BLK)
        k_pos = (i - 1) * BLK + jnp.arange(3 * BLK)
        valid = (jnp.abs(q_pos[:, None] - k_pos[None, :]) <= WINDOW) & (k_pos >= 0)[None, :] & (k_pos < S)[None, :]
        s_loc = jnp.einsum('bqhgd,bkhd->bhgqk', qs, ks).astype(jnp.float32) * scale
        s_loc = jnp.where(valid, s_loc, NEG_INF)
        s_ctx = jnp.einsum('bqhgd,bkhd->bhgqk', qs, kc).astype(jnp.float32) * scale
        sink = jnp.broadcast_to(sink_logit[None, :, :, None, None], s_loc.shape[:-1] + (1,))
        p = jax.nn.softmax(jnp.concatenate([s_loc, s_ctx, sink], axis=-1), axis=-1).astype(dt)
        n_loc = 3 * BLK
        return (jnp.einsum('bhgqk,bkhd->bqhgd', p[..., :n_loc], vs)
                + jnp.einsum('bhgqk,bkhd->bqhgd', p[..., n_loc:-1], vc))

    o = lax.map(block, jnp.arange(S // BLK))
    o = jnp.moveaxis(o, 0, 1).reshape(bsz, S, A_WIDTH)
    y = (o * jax.nn.silu(g)) @ w_o
    if not ctx_out:
        return y, None
    C = hc.shape[1]
    qc = (hc @ w_in[:, :A_WIDTH]).reshape(bsz, C, A_KV_HEADS, A_GROUP, A_HEAD_DIM)
    gc = hc @ w_in[:, A_WIDTH + 2 * A_KV_WIDTH:]
    s = jnp.einsum('bqhgd,bkhd->bhgqk', qc, kc).astype(jnp.float32) * scale
    sink = jnp.broadcast_to(sink_logit[None, :, :, None, None], s.shape[:-1] + (1,))
    p = jax.nn.softmax(jnp.concatenate([s, sink], axis=-1), axis=-1)[..., :-1].astype(dt)
    oc = jnp.einsum('bhgqk,bkhd->bqhgd', p, vc).reshape(bsz, C, A_WIDTH)
    yc = (oc * jax.nn.silu(gc)) @ w_o
    return y, yc


def mixer_b(h, hc, w_in, q_norm_g, w_uq, kv_norm_g, w_ukv, w_o, cos, sin, ctx_out):
    bsz, S, _ = h.shape
    C = hc.shape[1]
    dt = h.dtype
    splits = [B_Q_RANK, B_Q_RANK + B_KV_RANK, B_Q_RANK + B_KV_RANK + B_ROPE]
    cq, ckv, kr, g = jnp.split(h @ w_in, splits, axis=-1)
    q = (rmsnorm(cq, q_norm_g) @ w_uq).reshape(bsz, S, B_HEADS, B_NOPE + B_ROPE)
    qn, qr = q[..., :B_NOPE], apply_rope(q[..., B_NOPE:], cos, sin)
    kv = (rmsnorm(ckv, kv_norm_g) @ w_ukv).reshape(bsz, S, B_HEADS, B_NOPE + B_V)
    kn, v = kv[..., :B_NOPE], kv[..., B_NOPE:]
    kr = apply_rope(kr, cos, sin)
    ckv_c, kr_c = jnp.split(hc @ w_in[:, B_Q_RANK:splits[2]], [B_KV_RANK], axis=-1)
    kv_c = (rmsnorm(ckv_c, kv_norm_g) @ w_ukv).reshape(bsz, C, B_HEADS, B_NOPE + B_V)
    kn_c, v_c = kv_c[..., :B_NOPE], kv_c[..., B_NOPE:]
    kn_all = jnp.concatenate([kn_c, kn], axis=1)
    kr_all = jnp.concatenate([kr_c, kr], axis=1)
    v_all = jnp.concatenate([v_c, v], axis=1)
    scale = (B_NOPE + B_ROPE) ** -0.5

    def block(i):
        qn_b = lax.dynamic_slice_in_dim(qn, i * BLK, BLK, axis=1)
        qr_b = lax.dynamic_slice_in_dim(qr, i * BLK, BLK, axis=1)
        s = (jnp.einsum('bqhd,bkhd->bhqk', qn_b, kn_all)
             + jnp.einsum('bqhr,bkr->bhqk', qr_b, kr_all)).astype(jnp.float32) * scale
        p = jax.nn.softmax(s, axis=-1).astype(dt)
        return jnp.einsum('bhqk,bkhd->bqhd', p, v_all)

    o = lax.map(block, jnp.arange(S // BLK))
    o = jnp.moveaxis(o, 0, 1).reshape(bsz, S, B_WIDTH)
    y = (o * jax.nn.silu(g)) @ w_o
    if not ctx_out:
        return y, None
    qc = (rmsnorm(hc @ w_in[:, :B_Q_RANK], q_norm_g) @ w_uq).reshape(bsz, C, B_HEADS, B_NOPE + B_ROPE)
    gc = hc @ w_in[:, splits[2]:]
    s = (jnp.einsum('bqhd,bkhd->bhqk', qc[..., :B_NOPE], kn_c)
         + jnp.einsum('bqhr,bkr->bhqk', qc[..., B_NOPE:], kr_c)).astype(jnp.float32) * scale
    p = jax.nn.softmax(s, axis=-1).astype(dt)
    oc = jnp.einsum('bhqk,bkhd->bqhd', p, v_c).reshape(bsz, C, B_WIDTH)
    yc = (oc * jax.nn.silu(gc)) @ w_o
    return y, yc


def setup_inputs(seed: int = 0) -> dict:
    key = jax.random.key(seed)
    ks = jax.random.split(key, 24)
    f = jnp.float32
    D = D_MODEL

    def w(k, shape, fan_in):
        return jax.random.normal(k, shape, f) * fan_in ** -0.5

    def gain(k, n):
        return 1.0 + 0.02 * jax.random.normal(k, (n,), f)

    return {
        "x": jax.random.normal(ks[0], (BATCH, SEQ, D), f),
        "c": jax.random.normal(ks[1], (BATCH, D), f),
        "ctx": jax.random.normal(ks[2], (BATCH, CTX_LEN, D), f),
        "c_ctx": jax.random.normal(ks[3], (D,), f),
        "norm_g_0": gain(ks[4], D),
        "ada_w_0": w(ks[5], (D, 3 * D), D),
        "ada_b_0": 0.02 * jax.random.normal(ks[6], (3 * D,), f),
        "a_w_in_0": w(ks[7], (D, A_IN), D),
        "a_sinks_0": 0.5 * jax.random.normal(ks[8], (A_HEADS,), f),
        "a_w_o_0": w(ks[9], (A_WIDTH, D), A_WIDTH),
        "norm_g_1": gain(ks[10], D),
        "ada_w_1": w(ks[11], (D, 3 * D), D),
        "ada_b_1": 0.02 * jax.random.normal(ks[12], (3 * D,), f),
        "b_w_in_1": w(ks[13], (D, B_IN), D),
        "b_q_norm_1": gain(ks[14], B_Q_RANK),
        "b_w_uq_1": w(ks[15], (B_Q_RANK, B_HEADS * (B_NOPE + B_ROPE)), B_Q_RANK),
        "b_kv_norm_1": gain(ks[16], B_KV_RANK),
        "b_w_ukv_1": w(ks[17], (B_KV_RANK, B_HEADS * (B_NOPE + B_V)), B_KV_RANK),
        "b_w_o_1": w(ks[18], (B_WIDTH, D), B_WIDTH),
        "final_g": gain(ks[19], D),
    }


def reference(x, c, ctx, c_ctx, norm_g_0, ada_w_0, ada_b_0, a_w_in_0, a_sinks_0, a_w_o_0,
              norm_g_1, ada_w_1, ada_b_1, b_w_in_1, b_q_norm_1, b_w_uq_1, b_kv_norm_1,
              b_w_ukv_1, b_w_o_1, final_g):
    S = x.shape[1]
    cos_a, sin_a = axial_rope_tables(S, A_HEAD_DIM, x.dtype)
    cos_b, sin_b = axial_rope_tables(S, B_ROPE, x.dtype)
    layers = [
        (norm_g_0, ada_w_0, ada_b_0, (a_w_in_0, a_sinks_0, a_w_o_0)),
        (norm_g_1, ada_w_1, ada_b_1, (b_w_in_1, b_q_norm_1, b_w_uq_1, b_kv_norm_1, b_w_ukv_1, b_w_o_1)),
    ]
    for i in range(DEPTH):
        g_norm, ada_w, ada_b, mp = layers[i]
        last = i == DEPTH - 1
        shift, scale, gate = jnp.split(jax.nn.silu(c) @ ada_w + ada_b, 3, axis=-1)
        shift_c, scale_c, gate_c = jnp.split(jax.nn.silu(c_ctx) @ ada_w + ada_b, 3, axis=-1)
        h = modulate(x, g_norm, shift[:, None], scale[:, None])
        hc = modulate(ctx, g_norm, shift_c, scale_c)
        if i % N_MIXERS == 0:
            y, yc = mixer_a(h, hc, *mp, cos_a, sin_a, not last)
        else:
            y, yc = mixer_b(h, hc, *mp, cos_b, sin_b, not last)
        x = x + gate[:, None] * y
        if not last:
            ctx = ctx + gate_c * yc
    return rmsnorm(x, final_g)
```

```python
import numpy as np
import ml_dtypes
from contextlib import ExitStack
import concourse.bass as bass
import concourse.mybir as mybir
from concourse.bass_utils import run_bass_kernel_spmd

F32 = mybir.dt.float32
BF16 = mybir.dt.bfloat16
AF = mybir.ActivationFunctionType
ALU = mybir.AluOpType

ENGS = ("pe", "act", "dve", "pool", "sp")
SELF_SYNC = True


class Tile:
    def __init__(self, handle, name):
        self.h = handle
        self.name = name
        self.w = None
        self.dma_w = 0
        self.reads = {}
        self.pend_r = []
        self.dsem = None
        self.dcnt = 0

    def __getitem__(self, key):
        return View(self, self.h[key])


class View:
    def __init__(self, tile, ap):
        self.tile = tile
        self.ap = ap

    def __getitem__(self, key):
        return View(self.tile, self.ap[key])

    def rearrange(self, s, **kw):
        return View(self.tile, self.ap.rearrange(s, **kw))

    def unsqueeze(self, ax):
        return View(self.tile, self.ap.unsqueeze(ax))

    def to_broadcast(self, shape):
        return View(self.tile, self.ap.to_broadcast(list(shape)))


class Prog:
    def __init__(self, nc, stack):
        self.nc = nc
        self.stack = stack
        self.gstack = stack
        self.q = {e: [] for e in ENGS}
        self.cnt = {e: 0 for e in ENGS}
        self.seen = {e: {} for e in ENGS}
        self.sem = {e: stack.enter_context(nc.semaphore("s_" + e)) for e in ENGS}
        self.nsem = len(ENGS)
        self.ninst = 0
        self.final_dma = []
        self.uid = 0
        self.dma_tiles = []
        self.bar_tile = Tile(stack.enter_context(nc.sbuf_tensor("bar_t", [128, 1], F32)), "bar_t")

    def sb(self, shape, dt, name):
        self.uid += 1
        nm = "%s_%d" % (name, self.uid)
        return Tile(self.stack.enter_context(self.nc.sbuf_tensor(nm, list(shape), dt)), nm)

    def ps(self, shape, dt, name):
        self.uid += 1
        nm = "%s_%d" % (name, self.uid)
        return Tile(self.stack.enter_context(self.nc.psum_tensor(nm, list(shape), dt)), nm)

    def dram(self, handle, name):
        return Tile(handle, name)

    def _dsem(self, t):
        if t.dsem is None:
            t.dsem = self.gstack.enter_context(self.nc.semaphore("d_" + t.name))
            self.nsem += 1
            self.dma_tiles.append(t)
        return t.dsem

    def _wait(self, e, key, sem, val):
        if val <= 0 or self.seen[e].get(key, 0) >= val:
            return
        self.seen[e][key] = val
        self.q[e].append(("wait", sem, val))

    def _deps(self, e, reads, writes):
        for t in reads:
            if t.w is not None:
                we, wc = t.w
                if we != e or SELF_SYNC:
                    self._wait(e, we, self.sem[we], wc)
            if t.dma_w:
                self._wait(e, id(t), t.dsem, t.dma_w)
        for t in writes:
            if t.w is not None:
                we, wc = t.w
                if we != e:
                    self._wait(e, we, self.sem[we], wc)
            for re_, rc in t.reads.items():
                if re_ != e or SELF_SYNC:
                    self._wait(e, re_, self.sem[re_], rc)
            if t.dma_w:
                self._wait(e, id(t), t.dsem, t.dma_w)
            for key, sem, val in t.pend_r:
                self._wait(e, key, sem, val)

    @staticmethod
    def _tiles(vs):
        out = []
        for v in vs:
            if v is None:
                continue
            t = v.tile if isinstance(v, View) else v
            if t not in out:
                out.append(t)
        return out

    def op(self, e, fn, reads, writes):
        reads = self._tiles(reads)
        writes = self._tiles(writes)
        self._deps(e, reads, writes)
        self.cnt[e] += 1
        c = self.cnt[e]
        self.q[e].append(("op", fn))
        for t in reads:
            t.reads[e] = c
        for t in writes:
            t.w = (e, c)
            t.reads = {}
            t.dma_w = 0
            t.pend_r = []
        self.ninst += 1

    def dma(self, e, out, in_, final=False):
        o_ap = out.ap if isinstance(out, View) else out
        i_ap = in_.ap if isinstance(in_, View) else in_
        ot = out.tile if isinstance(out, View) else None
        it = in_.tile if isinstance(in_, View) else None
        self._deps(e, [it] if it is not None else [], [ot] if ot is not None else [])
        t = ot if ot is not None else it
        sem = self._dsem(t)
        t.dcnt += 16
        val = t.dcnt
        self.q[e].append(("dma", o_ap, i_ap, [sem]))
        if ot is not None:
            ot.w = None
            ot.reads = {}
            ot.dma_w = val
            ot.pend_r = []
        if it is not None:
            it.pend_r.append((id(t), sem, val))
        if final:
            self.final_dma.append((sem, val, id(t)))
        self.ninst += 1

    def all_gather(self, out_tile, in_tile, groups):
        e = "pool"
        self._deps(e, [in_tile], [out_tile])
        sem = self.gstack.enter_context(self.nc.semaphore("cc_sem"))
        self.nsem += 1
        out_tile.dsem = sem
        out_tile.dcnt = 1
        self.dma_tiles.append(out_tile)

        def fn(eng):
            eng.collective_compute("AllGather", ALU.bypass, replica_groups=groups,
                                   ins=[in_tile.h.ap()], outs=[out_tile.h.ap()]).then_inc(sem)

        self.q[e].append(("raw", fn))
        out_tile.w = None
        out_tile.reads = {}
        out_tile.dma_w = 1
        out_tile.pend_r = []
        in_tile.pend_r.append((id(out_tile), sem, 1))
        self.ninst += 1

    def barrier(self):
        for f in ENGS:
            if f != "dve":
                self._wait("dve", f, self.sem[f], self.cnt[f])
        for t in self.dma_tiles:
            if t.dcnt:
                self._wait("dve", id(t), t.dsem, t.dcnt)
        self.memset("dve", self.bar_tile[:, :], 0.0)
        c = self.cnt["dve"]
        for e in ENGS:
            if e != "dve":
                self._wait(e, "dve", self.sem["dve"], c)

    def flush(self, last=False):
        nc = self.nc
        if last:
            for sem, val, key in self.final_dma:
                self._wait("sp", key, sem, val)
        q = self.q
        sems = self.sem

        def run(engname, eng):
            mysem = sems[engname]
            for item in q[engname]:
                if item[0] == "wait":
                    eng.wait_ge(item[1], item[2])
                elif item[0] == "op":
                    item[1](eng).then_inc(mysem, 1)
                elif item[0] == "dma":
                    _, o_ap, i_ap, sl = item
                    ins = eng.dma_start(out=o_ap, in_=i_ap)
                    for s in sl:
                        ins = ins.then_inc(s, 16)
                else:
                    item[1](eng)

        with nc.Block() as block:
            @block.tensor
            def _(eng):
                run("pe", eng)

            @block.scalar
            def _(eng):
                run("act", eng)

            @block.vector
            def _(eng):
                run("dve", eng)

            @block.gpsimd
            def _(eng):
                run("pool", eng)

            @block.sync
            def _(eng):
                run("sp", eng)
        self.q = {e: [] for e in ENGS}

    def mm(self, out, lhsT, rhs, start=True, stop=True):
        self.op("pe", lambda eng: eng.matmul(out.ap, lhsT.ap, rhs.ap, start=start, stop=stop),
                [lhsT, rhs], [out])

    def transpose(self, out, in_, ident):
        self.op("pe", lambda eng: eng.transpose(out.ap, in_.ap, ident.ap), [in_, ident], [out])

    def act(self, out, in_, func, bias=None, scale=None, accum_out=None):
        kw = {}
        reads = [in_]
        writes = [out]
        if bias is not None:
            kw["bias"] = bias.ap if isinstance(bias, View) else bias
            if isinstance(bias, View):
                reads.append(bias)
        if scale is not None:
            kw["scale"] = scale.ap if isinstance(scale, View) else scale
            if isinstance(scale, View):
                reads.append(scale)
        if accum_out is not None:
            kw["accum_out"] = accum_out.ap
            writes.append(accum_out)
        self.op("act", lambda e: e.activation(out.ap, in_.ap, func, **kw), reads, writes)

    def copy(self, e, out, in_):
        if e == "act":
            self.op(e, lambda eng: eng.copy(out.ap, in_.ap), [in_], [out])
        else:
            self.op(e, lambda eng: eng.tensor_copy(out.ap, in_.ap), [in_], [out])

    def tt(self, e, out, in0, in1, op):
        self.op(e, lambda eng: eng.tensor_tensor(out.ap, in0.ap, in1.ap, op), [in0, in1], [out])

    def ts(self, e, out, in0, s1, s2, op0, op1=None):
        reads = [in0]
        a1 = s1.ap if isinstance(s1, View) else s1
        a2 = s2.ap if isinstance(s2, View) else s2
        if isinstance(s1, View):
            reads.append(s1)
        if isinstance(s2, View):
            reads.append(s2)
        kw = {}
        if op1 is not None:
            kw["op1"] = op1
        self.op(e, lambda eng: eng.tensor_scalar(out.ap, in0.ap, a1, a2, op0, **kw), reads, [out])

    def stt(self, e, out, in0, scalar, in1, op0, op1):
        reads = [in0, in1]
        a = scalar.ap if isinstance(scalar, View) else scalar
        if isinstance(scalar, View):
            reads.append(scalar)
        self.op(e, lambda eng: eng.scalar_tensor_tensor(out.ap, in0.ap, a, in1.ap, op0, op1), reads, [out])

    def memset(self, e, out, val):
        self.op(e, lambda eng: eng.memset(out.ap, val), [], [out])

    def recip(self, out, in_):
        self.op("dve", lambda eng: eng.reciprocal(out.ap, in_.ap), [in_], [out])


class Ring:
    def __init__(self, tiles):
        self.tiles = tiles
        self.i = 0

    def next(self):
        t = self.tiles[self.i % len(self.tiles)]
        self.i += 1
        return t


D = 1024
SEQ = 8192
NCORE = 8
TOK = 2048
NB = TOK // 128
CTX = 256
EPS = 1e-6
GRID_W = 64
ROPE_BASE = 10000.0


def rope_tables(pos, rot_dim):
    pos = np.asarray(pos)
    row = (pos // GRID_W).astype(np.float32)
    col = (pos % GRID_W).astype(np.float32)
    nf = rot_dim // 4
    inv = (np.float32(ROPE_BASE) ** (-np.arange(nf, dtype=np.float32) / np.float32(nf))).astype(np.float32)
    ar = row[:, None] * inv
    ac = col[:, None] * inv
    ang = np.concatenate([ar, ar, ac, ac], axis=-1).astype(np.float32)
    cos = np.cos(ang).astype(np.float32)
    sin = np.sin(ang).astype(np.float32)
    sgn = np.concatenate([-np.ones(nf), np.ones(nf), -np.ones(nf), np.ones(nf)]).astype(np.float32)
    return cos, (sin * sgn).astype(np.float32)


class Ctx:
    pass


class Phase:
    def __init__(self, P):
        self.P = P

    def __enter__(self):
        self.prev = self.P.stack
        self.st = ExitStack()
        self.st.__enter__()
        self.P.stack = self.st
        return self

    def __exit__(self, *a):
        self.P.barrier()
        self.P.flush()
        self.P.stack = self.prev
        return self.st.__exit__(*a)


def emit_rms_rstd(P, C, x_view, n, ss, rstd):
    P.memset("pool", ss[:, 0:1], 0.0)
    P.act(C.junk[:, 0:n], x_view, AF.Square, accum_out=ss[:, 0:1])
    P.act(rstd[:, 0:1], ss[:, 0:1], AF.Ln, bias=C.eps_t[:, 0:1], scale=1.0 / n)
    P.act(rstd[:, 0:1], rstd[:, 0:1], AF.Exp, scale=-0.5)


def emit_modulate(P, C, x_view, A, B, hb):
    ss = C.ss.next()
    rstd = C.rstd.next()
    emit_rms_rstd(P, C, x_view, D, ss, rstd)
    t = C.t32.next()
    P.stt("dve", t[:, :], x_view, rstd[:, 0:1], A[:, :], ALU.mult, ALU.mult)
    P.tt("dve", hb[:, :], t[:, :], B[:, :], ALU.add)


def emit_hT(P, C, hb, hT):
    ps = C.psT
    for k in range(8):
        P.transpose(ps[:, k, :], hb[:, k * 128:(k + 1) * 128], C.ident[:, :])
    P.copy("dve", hT[:, :, :], ps[:, :, :])


def emit_proj(P, hT, w, c0, c1, ps):
    for k in range(8):
        P.mm(ps[:, 0:c1 - c0], hT[:, k, :], w[:, k, c0:c1], start=(k == 0), stop=(k == 7))


def emit_rope_tm(P, C, src, nh, hd, cos_v, sin_v, dst):
    q = hd // 4
    t1 = C.t32.next()
    t2 = C.t32.next()
    n = nh * hd
    s3 = src.rearrange("p (h d) -> p h d", h=nh)
    P.tt("dve", t1[:, 0:n].rearrange("p (h d) -> p h d", h=nh), s3,
         cos_v.unsqueeze(1).to_broadcast([128, nh, hd]), ALU.mult)
    s5 = src.rearrange("p (h a b d) -> p h a b d", h=nh, a=2, b=2, d=q)
    t5 = t2[:, 0:n].rearrange("p (h a b d) -> p h a b d", h=nh, a=2, b=2, d=q)
    sn = sin_v.rearrange("p (a b d) -> p a b d", a=2, b=2, d=q)
    for half in range(2):
        P.tt("pool", t5[:, :, :, half, :], s5[:, :, :, 1 - half, :],
             sn[:, :, half, :].unsqueeze(1).to_broadcast([128, nh, 2, q]), ALU.mult)
    P.tt("dve", dst, t1[:, 0:n], t2[:, 0:n], ALU.add)


def emit_head_transposes(P, C, src_bf, nheads, dst):
    for g0 in range(0, nheads, 8):
        ps = C.psH
        ng = min(8, nheads - g0)
        for g in range(ng):
            h = g0 + g
            P.transpose(ps[0:64, g, :], src_bf[:, h * 64:(h + 1) * 64], C.ident[:, :])
        P.copy("dve", dst[:, g0:g0 + ng, :], ps[0:64, 0:ng, :])


def emit_silu_from_psum(P, C, ps_view, n, dst_bf):
    u = C.t32.next()
    P.act(u[:, 0:n], ps_view, AF.Exp, scale=-1.0)
    P.act(u[:, 0:n], u[:, 0:n], AF.Ln, bias=C.one_t[:, 0:1])
    P.act(u[:, 0:n], u[:, 0:n], AF.Exp, scale=-1.0)
    P.tt("dve", dst_bf, ps_view, u[:, 0:n], ALU.mult)


def alloc_ada_tmp(P, C):
    C.cT = P.sb([128, 16], F32, "cT")
    C.cu = P.sb([128, 16], F32, "cu")
    C.cxh = P.sb([128, 16], F32, "cxh")
    C.lhs_b = P.sb([128, 8, 128], BF16, "lhs_b")
    C.lhs_c = P.sb([128, 8, 128], BF16, "lhs_c")
    C.ada_bias = Ring([P.sb([128, 128], F32, "ada_bias") for _ in range(2)])
    C.ng_bc = P.sb([128, D], F32, "ng_bc")
    C.adaw = Ring([P.sb([128, 8, 128], BF16, "adaw") for _ in range(2)])


def emit_silu_c(P, C, cT_ap):
    ct = C.cT
    P.dma("sp", ct[:, :], cT_ap)
    u = C.cu
    xh = C.cxh
    P.act(u[:, :], ct[:, :], AF.Exp, scale=-1.0)
    P.act(u[:, :], u[:, :], AF.Ln, bias=C.one_t[:, 0:1])
    P.act(u[:, :], u[:, :], AF.Exp, scale=-1.0)
    P.tt("dve", u[:, :], u[:, :], ct[:, :], ALU.mult)
    for r, lhs in ((0, C.lhs_b), (1, C.lhs_c)):
        for k in range(8):
            P.ts("dve", lhs[:, k, :], C.ones32[:, :], u[:, r * 8 + k:r * 8 + k + 1], None, ALU.mult)


def emit_ada(P, C, ada_w_ap, ada_b_ap, ng_ap, outs_b, outs_c, extra=None):
    ng = C.ng_bc
    P.dma("sp", ng[:, :], ng_ap.partition_broadcast(128))
    wv = ada_w_ap.rearrange("(k p) n -> p k n", p=128)
    NT = 128
    extra = list(extra or [])
    for n in range(3 * D // NT):
        wt = C.adaw.next()
        P.dma("pool", wt[:, :, :], wv[:, :, n * NT:(n + 1) * NT])
        if extra and n % 2 == 1:
            extra.pop(0)()
        bias = C.ada_bias.next()
        P.dma("sp", bias[:, :], ada_b_ap[:, n * NT:(n + 1) * NT].partition_broadcast(128))
        kind = (n * NT) // D
        sl = slice((n * NT) % D, (n * NT) % D + NT)
        for lhs, outs in ((C.lhs_b, outs_b), (C.lhs_c, outs_c)):
            if outs is None or outs[kind] is None:
                continue
            ps = C.psMM.next()
            for k in range(8):
                P.mm(ps[:, 0:NT], lhs[:, k, :], wt[:, k, :], start=(k == 0), stop=(k == 7))
            dst = outs[kind]
            if kind == 1:
                t = C.t32.next()
                P.tt("dve", t[:, 0:NT], ps[:, 0:NT], bias[:, :], ALU.add)
                P.stt("dve", dst[:, sl], t[:, 0:NT], 1.0, ng[:, sl], ALU.add, ALU.mult)
            else:
                P.tt("dve", dst[:, sl], ps[:, 0:NT], bias[:, :], ALU.add)
    for f in extra:
        f()


def emit_attn_a(P, C, QT, sgT, keyblocks, ogT, only_kv=None):
    for kv in (range(4) if only_kv is None else [only_kv]):
        q_rhs = QT[:, kv * 4:(kv + 1) * 4, :].rearrange("p g q -> p (g q)")
        psO = C.psO.next()
        nk = len(keyblocks)

        def front(bi):
            KT, VA, mask = keyblocks[bi]
            psS = C.psS.next()
            P.mm(psS[:, :], KT[:, kv, :], q_rhs)
            pt = C.pT.next()
            P.act(pt[:, :], psS[:, :], AF.Exp, scale=0.125)
            if mask is not None:
                P.tt("dve", pt[:, :].rearrange("p (g q) -> p g q", g=4),
                     pt[:, :].rearrange("p (g q) -> p g q", g=4),
                     mask.unsqueeze(1).to_broadcast([128, 4, 128]), ALU.mult)
            return pt

        def back(bi, pt):
            KT, VA, mask = keyblocks[bi]
            P.mm(psO[:, :], VA[:, kv, :], pt[:, :], start=(bi == 0), stop=False)

        prev = None
        for bi in range(nk):
            pt = front(bi)
            if prev is not None:
                back(*prev)
            prev = (bi, pt)
        back(*prev)
        P.mm(psO[:, :], C.sel[0:1, :], C.esink[0:1, kv * 512:(kv + 1) * 512], start=False, stop=True)
        rec = C.rec.next()
        P.act(rec[0:64, :], psO[64:128, :], AF.Ln)
        P.act(rec[0:64, :], rec[0:64, :], AF.Exp, scale=-1.0)
        o = C.o32.next()
        P.tt("dve", o[0:64, :], psO[0:64, :], rec[0:64, :], ALU.mult)
        o4 = o[0:64, :].rearrange("p (c e q) -> p c e q", c=2, e=2)
        s4 = sgT[:, kv * 4:(kv + 1) * 4, :].rearrange("p (c e) q -> p c e q", c=2)
        for par in range(2):
            P.tt("pool", ogT[par * 64:(par + 1) * 64, kv * 2:kv * 2 + 2, :], o4[:, :, par, :], s4[:, :, par, :], ALU.mult)


def emit_out_a(P, C, ogT, w_o, G, x_tile, x1):
    for n in range(2):
        ps = C.psO.next()
        for k in range(8):
            P.mm(ps[:, :], ogT[:, k, :], w_o[:, k, n * 512:(n + 1) * 512], start=(k == 0), stop=(k == 7))
        t = C.ty.next()
        sl = slice(n * 512, (n + 1) * 512)
        P.tt("dve", t[:, 0:512], ps[:, :], G[:, sl], ALU.mult)
        P.tt("pool", x1[:, sl], t[:, 0:512], x_tile[:, sl], ALU.add)


def l0_part1a(P, C, x_view, A, B):
    hb = C.hb.next()
    emit_modulate(P, C, x_view, A, B, hb)
    hT = C.hT.next()
    emit_hT(P, C, hb, hT)
    return hT


def l0_part1b(P, C, hT, cos_v, sin_v, KT, VA):
    ps = C.psMM.next()
    emit_proj(P, hT, C.w_in0, 1024, 1536, ps)
    kr = C.kr.next()
    if cos_v is not None:
        k32 = C.t32.next()
        P.copy("act", k32[:, 0:256], ps[:, 0:256])
        emit_rope_tm(P, C, k32[:, 0:256], 4, 64, cos_v, sin_v, kr[:, :])
    else:
        P.copy("act", kr[:, :], ps[:, 0:256])
    P.copy("act", VA[:, :, 0:64], ps[:, 256:512].rearrange("p (h d) -> p h d", h=4))
    P.memset("pool", VA[:, :, 64:128], 1.0)
    emit_head_transposes(P, C, kr, 4, KT)


def l0_part1(P, C, x_view, A, B, cos_v, sin_v, KT, VA):
    hT = l0_part1a(P, C, x_view, A, B)
    l0_part1b(P, C, hT, cos_v, sin_v, KT, VA)
    return hT


def l0_part2q(P, C, hT, cos_v, sin_v, QT):
    q32 = C.q32
    for n in range(2):
        ps = C.psMM.next()
        emit_proj(P, hT, C.w_in0, n * 512, (n + 1) * 512, ps)
        P.copy("act", q32[:, n * 512:(n + 1) * 512], ps[:, :])
    qr = C.qr
    if cos_v is not None:
        emit_rope_tm(P, C, q32[:, :], 16, 64, cos_v, sin_v, qr[:, :])
    else:
        P.copy("pool", qr[:, :], q32[:, :])
    emit_head_transposes(P, C, qr, 16, QT)


def l0_part2g(P, C, hT, sgT):
    sg = C.sg
    for n in range(2):
        ps = C.psMM.next()
        emit_proj(P, hT, C.w_in0, 1536 + n * 512, 1536 + (n + 1) * 512, ps)
        emit_silu_from_psum(P, C, ps[:, :], 512, sg[:, n * 512:(n + 1) * 512])
    emit_head_transposes(P, C, sg, 16, sgT)


def l0_part2(P, C, hT, cos_v, sin_v, QT, sgT):
    l0_part2q(P, C, hT, cos_v, sin_v, QT)
    l0_part2g(P, C, hT, sgT)


def alloc_l0_set(P, C):
    C.w_in0 = P.sb([128, 8, 2560], BF16, "w_in0")
    C.w_o0 = P.sb([128, 8, 1024], BF16, "w_o0")
    C.masks = P.sb([128, 512], BF16, "masks")
    C.cosA = P.sb([128, NB + 2, 64], F32, "cosA")
    C.sinA = P.sb([128, NB + 2, 64], F32, "sinA")
    C.sel = P.sb([1, 128], BF16, "sel")
    C.sink32 = P.sb([1, 16], F32, "sink32")
    C.esink = P.sb([1, 2048], BF16, "esink")
    C.hb = Ring([P.sb([128, D], BF16, "hb") for _ in range(2)])
    C.hT = Ring([P.sb([128, 8, 128], BF16, "hT") for _ in range(3)])
    C.ty = Ring([P.sb([128, 512], F32, "ty") for _ in range(2)])
    C.kr = Ring([P.sb([128, 256], BF16, "kr") for _ in range(2)])
    C.q32 = P.sb([128, D], F32, "q32")
    C.qr = P.sb([128, D], BF16, "qr")
    C.sg = P.sb([128, D], BF16, "sg")
    C.QT = Ring([P.sb([64, 16, 128], BF16, "QT") for _ in range(3)])
    C.sgT = Ring([P.sb([64, 16, 128], BF16, "sgT") for _ in range(3)])
    C.KTc = [P.sb([64, 4, 128], BF16, "KTc") for _ in range(2)]
    C.VAc = [P.sb([128, 4, 128], BF16, "VAc") for _ in range(2)]
    C.pT = Ring([P.sb([128, 512], BF16, "pT") for _ in range(3)])
    C.rec = Ring([P.sb([64, 512], F32, "rec") for _ in range(2)])
    C.o32 = Ring([P.sb([64, 512], F32, "o32") for _ in range(1)])
    C.ogT = Ring([P.sb([128, 8, 128], BF16, "ogT") for _ in range(2)])
    C.x1 = Ring([P.sb([128, D], F32, "x1") for _ in range(2)])
    C.psT = P.ps([128, 8, 128], BF16, "psT")
    C.psH = P.ps([128, 8, 128], BF16, "psH")
    C.psMM = Ring([P.ps([128, 512], F32, "psMM") for _ in range(2)])
    C.psS = Ring([P.ps([128, 512], F32, "psS") for _ in range(2)])
    C.psO = Ring([P.ps([128, 512], F32, "psO") for _ in range(2)])


def alloc_globals(P, C):
    C.ident = P.sb([128, 128], BF16, "ident")
    C.ones32 = P.sb([128, 128], F32, "ones32")
    C.junk = P.sb([128, D], BF16, "junk")
    C.ss = Ring([P.sb([128, 1], F32, "ss") for _ in range(4)])
    C.rstd = Ring([P.sb([128, 1], F32, "rstd") for _ in range(4)])
    C.t32 = Ring([P.sb([128, D], F32, "t32") for _ in range(3)])
    C.eps_t = P.sb([128, 1], F32, "eps_t")
    P.memset("pool", C.eps_t[:, :], EPS)
    C.one_t = P.sb([128, 1], F32, "one_t")
    P.memset("pool", C.one_t[:, :], 1.0)
    C.A_b = P.sb([128, D], F32, "A_b")
    C.B_b = P.sb([128, D], F32, "B_b")
    C.G_b = P.sb([128, D], F32, "G_b")


def build_layer0(P, C, io):
    with Phase(P):
        alloc_l0_set(P, C)
        w_in = C.w_in0
        w_o = C.w_o0
        wv = io.w_in0.rearrange("(k p) n -> p k n", p=128)
        wov = io.w_o0.rearrange("(k p) n -> p k n", p=128)
        wloads = [(lambda k=k: P.dma("pool", w_in[:, k, :], wv[:, k, :])) for k in range(8)]
        wloads += [(lambda k0=k0: P.dma("pool", w_o[:, k0:k0 + 4, :], wov[:, k0:k0 + 4, :])) for k0 in (0, 4)]
        P.dma("pool", C.ident[:, :], io.ident)
        P.dma("pool", C.masks[:, :], io.masks)
        P.dma("sp", C.cosA[:, :, :], io.cosA.rearrange("(n p) d -> p n d", p=128))
        P.dma("sp", C.sinA[:, :, :], io.sinA.rearrange("(n p) d -> p n d", p=128))
        P.memset("pool", C.ones32[:, :], 1.0)
        P.memset("pool", C.sel[:, 0:64], 0.0)
        P.memset("pool", C.sel[:, 64:128], 1.0)
        P.dma("sp", C.sink32[0:1, :], io.sinks)
        P.act(C.sink32[0:1, :], C.sink32[0:1, :], AF.Exp)
        P.copy("dve", C.esink[0:1, :].rearrange("p (h q) -> p h q", h=16),
               C.sink32[0:1, :].unsqueeze(2).to_broadcast([1, 16, 128]))

        ctx_keys = [(C.KTc[0], C.VAc[0], None), (C.KTc[1], C.VAc[1], None)]
        with Phase(P):
            alloc_ada_tmp(P, C)
            A_c = P.sb([128, D], F32, "A_c")
            B_c = P.sb([128, D], F32, "B_c")
            G_c = P.sb([128, D], F32, "G_c")
            cxt = [P.sb([128, D], F32, "cx") for _ in range(2)]
            emit_silu_c(P, C, io.cT)
            emit_ada(P, C, io.ada_w0, io.ada_b0, io.ng0, (C.B_b, C.A_b, C.G_b), (B_c, A_c, G_c), extra=wloads)
            ctxv = io.ctx.rearrange("(n p) d -> p n d", p=128)
            hTs = []
            for cb in range(2):
                P.dma("sp", cxt[cb][:, :], ctxv[:, cb, :])
                hTs.append(l0_part1(P, C, cxt[cb][:, :], A_c, B_c, None, None, C.KTc[cb], C.VAc[cb]))
            ctx1v = io.ctx1.rearrange("(n p) d -> p n d", p=128)
            for cb in range(2):
                QT = C.QT.next()
                sgT = C.sgT.next()
                l0_part2(P, C, hTs[cb], None, None, QT, sgT)
                ogT = C.ogT.next()
                emit_attn_a(P, C, QT, sgT, ctx_keys, ogT)
                x1 = C.x1.next()
                emit_out_a(P, C, ogT, w_o, G_c, cxt[cb], x1)
                P.dma("pool", ctx1v[:, cb, :], x1[:, :], final=io.final_x1)
                if io.on_ctx1 is not None:
                    io.on_ctx1(cb, x1)

        with Phase(P):
            xin = Ring([P.sb([128, D], F32, "xin") for _ in range(6)])
            KTr = Ring([P.sb([64, 4, 128], BF16, "KT") for _ in range(5)])
            VAr = Ring([P.sb([128, 4, 128], BF16, "VA") for _ in range(5)])
            xv = io.xh.rearrange("(n p) d -> p n d", p=128)
            x1v = io.x1.rearrange("(n p) d -> p n d", p=128)
            xs, KTs, VAs, QTs, sgTs = {}, {}, {}, {}, {}

            def load(i):
                xs[i] = xin.next()
                P.dma("sp", xs[i][:, :], xv[:, i, :])

            def attn_chunk(j, kv, ogTs):
                mprev = C.masks[:, 256:384] if j == 1 else C.masks[:, 0:128]
                mnext = C.masks[:, 384:512] if j == NB else C.masks[:, 128:256]
                keys = [(KTs[j - 1], VAs[j - 1], mprev), (KTs[j], VAs[j], None),
                        (KTs[j + 1], VAs[j + 1], mnext)] + ctx_keys
                if kv == 0:
                    ogTs[j] = C.ogT.next()
                emit_attn_a(P, C, QTs[j], sgTs[j], keys, ogTs[j], only_kv=kv)

            def out_chunk(j, ogTs):
                x1 = C.x1.next()
                emit_out_a(P, C, ogTs[j], w_o, C.G_b, xs[j], x1)
                P.dma("pool", x1v[:, j - 1, :], x1[:, :], final=io.final_x1)

            ogTs = {}
            load(0)
            load(1)
            for i in range(NB + 4):
                j = i - 2
                doj = 1 <= j <= NB
                dox = i < NB + 2
                full = 1 <= i <= NB
                if i + 2 < NB + 2:
                    load(i + 2)
                if dox:
                    KTs[i] = KTr.next()
                    VAs[i] = VAr.next()
                    cv, sv = C.cosA[:, i, :], C.sinA[:, i, :]
                    hT = l0_part1a(P, C, xs[i][:, :], C.A_b, C.B_b)
                if doj:
                    attn_chunk(j, 0, ogTs)
                if dox:
                    l0_part1b(P, C, hT, cv, sv, KTs[i], VAs[i])
                if doj:
                    attn_chunk(j, 1, ogTs)
                if dox and full:
                    QTs[i] = C.QT.next()
                    l0_part2q(P, C, hT, cv, sv, QTs[i])
                if doj:
                    attn_chunk(j, 2, ogTs)
                if dox and full:
                    sgTs[i] = C.sgT.next()
                    l0_part2g(P, C, hT, sgTs[i])
                if doj:
                    attn_chunk(j, 3, ogTs)
                    out_chunk(j, ogTs)


def declare_inputs_a(nc, io):
    def din(name, shape):
        return nc.dram_tensor(name, list(shape), F32, kind="ExternalInput").ap()

    io.xh = din("xh", [(NB + 2) * 128, D])
    io.ctx = din("ctx", [CTX, D])
    io.cT = din("cT", [128, 16])
    io.ada_w0 = din("ada_w0", [D, 3 * D])
    io.ada_b0 = din("ada_b0", [1, 3 * D])
    io.ng0 = din("ng0", [1, D])
    io.w_in0 = din("w_in0", [D, 2560])
    io.w_o0 = din("w_o0", [D, D])
    io.sinks = din("sinks", [1, 16])
    io.cosA = din("cosA", [(NB + 2) * 128, 64])
    io.sinA = din("sinA", [(NB + 2) * 128, 64])
    io.masks = din("masks", [128, 512])
    io.ident = din("ident", [128, 128])


def build_prog_a():
    nc = bass.Bass("TRN2", target_bir_lowering=False)
    io = Ctx()
    declare_inputs_a(nc, io)
    io.x1 = nc.dram_tensor("x1", [TOK, D], F32, kind="ExternalOutput").ap()
    io.ctx1 = nc.dram_tensor("ctx1", [CTX, D], F32, kind="ExternalOutput").ap()
    io.final_x1 = True
    io.on_x1 = None
    io.on_ctx1 = None
    with ExitStack() as st:
        P = Prog(nc, st)
        C = Ctx()
        alloc_globals(P, C)
        build_layer0(P, C, io)
        P.flush(last=True)
        print("prog A: ninst", P.ninst, "nsem", P.nsem)
    return nc


def host_inputs_a(inp, c):
    b, j = c // 4, c % 4
    t0 = TOK * j
    x = inp["x"]
    xh = np.zeros(((NB + 2) * 128, D), np.float32)
    lo, hi = t0 - 128, t0 + TOK + 128
    slo, shi = max(lo, 0), min(hi, SEQ)
    xh[slo - lo:shi - lo] = x[b, slo:shi]
    pos = np.clip(np.arange(lo, hi), 0, SEQ - 1)
    cosA, sinA = rope_tables(pos, 64)
    cT = np.zeros((128, 16), np.float32)
    cT[:, 0:8] = inp["c"][b].reshape(8, 128).T
    cT[:, 8:16] = inp["c_ctx"].reshape(8, 128).T
    kk = np.arange(128)[:, None]
    qq = np.arange(128)[None, :]
    tri_prev = (kk >= qq).astype(np.float32)
    tri_next = (kk <= qq).astype(np.float32)
    masks = np.concatenate([tri_prev, tri_next,
                            tri_prev if j > 0 else np.zeros_like(tri_prev),
                            tri_next if j < 3 else np.zeros_like(tri_next)], axis=1)
    return {
        "xh": xh, "ctx": np.ascontiguousarray(inp["ctx"][b]), "cT": cT,
        "ada_w0": inp["ada_w_0"], "ada_b0": inp["ada_b_0"].reshape(1, -1), "ng0": inp["norm_g_0"].reshape(1, -1),
        "w_in0": inp["a_w_in_0"], "w_o0": inp["a_w_o_0"], "sinks": inp["a_sinks_0"].reshape(1, -1),
        "cosA": cosA, "sinA": sinA, "masks": masks, "ident": np.eye(128, dtype=np.float32),
    }


def run_a(inp, trace=False):
    nc = build_prog_a()
    in_maps = [host_inputs_a(inp, c) for c in range(NCORE)]
    return run_bass_kernel_spmd(nc, in_maps, core_ids=list(range(NCORE)), trace=trace)


SCALE1 = 96.0 ** -0.5
NKEY = SEQ + CTX
NCH = 17


def emit_silu3(P, C, ps3, dst3, a, b):
    n = a * b
    u = C.t32.next()
    u3 = u[:, 0:n].rearrange("p (a b) -> p a b", a=a)
    P.act(u3, ps3, AF.Exp, scale=-1.0)
    P.act(u[:, 0:n], u[:, 0:n], AF.Ln, bias=C.one_t[:, 0:1])
    P.act(u[:, 0:n], u[:, 0:n], AF.Exp, scale=-1.0)
    P.tt("dve", dst3, ps3, u3, ALU.mult)


def alloc_l1_set(P, C):
    C.w_o1 = P.sb([128, 8, 1024], BF16, "w_o1")
    C.fg = P.sb([128, D], F32, "fg")
    C.cqnT = P.sb([128, 2, TOK], BF16, "cqnT")
    C.sgT = P.sb([128, 8, TOK], BF16, "sgT1")
    C.ckvnT = P.sb([128, NKEY], BF16, "ckvnT")
    C.KT1 = [P.sb([96, 512], BF16, "KT1") for _ in range(NCH)]
    C.psMM = Ring([P.ps([128, 512], F32, "psMM1") for _ in range(2)])


def l1_block(P, C, x_view, A, B, cos_v, sin_v, do_q, ka_dst, kb_dst, kb_shift, tok_sl, hT=None):
    if hT is None:
        hb = C.hb.next()
        emit_modulate(P, C, x_view, A, B, hb)
        hT = C.hT.next()
        emit_hT(P, C, hb, hT)
    c0 = 0 if do_q else 256
    ps = C.psMM.next()
    emit_proj(P, hT, C.w_in1, c0, 416, ps)
    off = 256 - c0
    ks = C.ks.next()
    ss = C.ss.next()
    rstd = C.rstd.next()
    emit_rms_rstd(P, C, ps[:, off:off + 128], 128, ss, rstd)
    P.stt("dve", ks[:, 0:128], ps[:, off:off + 128], rstd[:, 0:1], C.kvg[:, :], ALU.mult, ALU.mult)
    if cos_v is not None:
        k32 = C.t32.next()
        P.copy("act", k32[:, 0:32], ps[:, off + 128:off + 160])
        emit_rope_tm(P, C, k32[:, 0:32], 1, 32, cos_v, sin_v, ks[:, 128:160])
    else:
        P.copy("act", ks[:, 128:160], ps[:, off + 128:off + 160])
    pk = C.psK
    P.transpose(pk[:, 0, :], ks[:, 0:128], C.ident[:, :])
    P.transpose(pk[0:32, 1, :], ks[:, 128:160], C.ident[:, :])
    P.copy("dve", ka_dst, pk[:, 0, :])
    P.copy("dve", kb_dst, pk[0:32, 1, :])
    if not do_q:
        return
    ss = C.ss.next()
    rstd = C.rstd.next()
    emit_rms_rstd(P, C, ps[:, 0:256], 256, ss, rstd)
    cqn = C.cqn.next()
    P.stt("dve", cqn[:, :], ps[:, 0:256], rstd[:, 0:1], C.qg[:, :], ALU.mult, ALU.mult)
    pt = C.psT
    for c in range(2):
        P.transpose(pt[:, c, :], cqn[:, c * 128:(c + 1) * 128], C.ident[:, :])
    P.copy("dve", C.cqnT[:, :, tok_sl], pt[:, 0:2, :])
    for p0 in (0, 4):
        pg = C.psMM.next()
        pg3 = pg[:, :].rearrange("p (a b) -> p a b", a=4)
        for p in range(4):
            col = 416 + (p0 + p) * 128
            for k in range(8):
                P.mm(pg3[:, p, :], C.w_in1[:, k, col:col + 128], hT[:, k, :], start=(k == 0), stop=(k == 7))
        emit_silu3(P, C, pg3, C.sgT[:, p0:p0 + 4, tok_sl], 4, 128)


def l1_gen_kv(P, C, h, c):
    ncol = 512 if c < 16 else 256
    nkb = ncol // 128
    base = c * 512
    ps = C.psG.next()
    P.mm(ps[0:64, 0:ncol], C.w_ukv[:, h * 128:h * 128 + 64], C.ckvnT[:, base:base + ncol])
    P.copy("dve", C.KT1[c][0:64, 0:ncol], ps[0:64, 0:ncol])
    ps2 = C.psG.next()
    for kb in range(nkb):
        P.mm(ps2[:, kb * 64:(kb + 1) * 64], C.ckvnT[:, base + kb * 128:base + (kb + 1) * 128],
             C.w_ukv[:, h * 128 + 64:h * 128 + 128])
    voff = 0 if h % 2 == 0 else 64
    P.copy("dve", C.VA1[c][:, 0:nkb, voff:voff + 64], ps2[:, 0:nkb * 64].rearrange("p (k d) -> p k d", k=nkb))
    P.memset("pool", C.VA1[c][:, 0:nkb, 64 - voff:128 - voff], 1.0)


def l1_gen_q(P, C, h, qt, QTh):
    sl = slice(qt * 512, (qt + 1) * 512)
    psA = C.psG.next()
    for c in range(2):
        P.mm(psA[0:96, :], C.w_uq[:, c, h * 96:(h + 1) * 96], C.cqnT[:, c, sl], start=(c == 0), stop=(c == 1))
    P.copy("dve", QTh[0:64, sl], psA[0:64, :])
    t = C.rt.next()
    P.tt("dve", t[64:96, :], psA[64:96, :], C.cosT[64:96, sl], ALU.mult)
    psB = C.psG.next()
    for c in range(2):
        P.mm(psB[0:96, :], C.w_uqs[:, c, h * 96:(h + 1) * 96], C.cqnT[:, c, sl], start=(c == 0), stop=(c == 1))
    u = C.rt.next()
    P.tt("dve", u[64:96, :], psB[64:96, :], C.sinT[64:96, sl], ALU.mult)
    P.tt("pool", QTh[64:96, sl], t[64:96, :], u[64:96, :], ALU.add)


def l1_attn_all(P, C, nheads, QTr):
    groups = []
    for h in range(nheads):
        for qt in range(4):
            gi = 0
            for c in range(NCH):
                nkb = 4 if c < 16 else 2
                for g0 in range(0, nkb, 2):
                    groups.append((h, qt, c, g0, gi, g0 + 2 >= nkb))
                    gi += 1
    qbuf = {}
    qbuf[0] = QTr.next()
    for c in range(NCH):
        l1_gen_kv(P, C, 0, c)
    for qt in range(4):
        l1_gen_q(P, C, 0, qt, qbuf[0])
    state = {}

    def front(grp):
        h, qt, c, g0, gi, last_of_chunk = grp
        sl = slice(qt * 512, (qt + 1) * 512)
        if gi == 0:
            state[(h, qt)] = C.psO.next()
        psS = C.psS.next()
        for kb in range(2):
            P.mm(psS[:, kb, :], C.KT1[c][0:96, (g0 + kb) * 128:(g0 + kb + 1) * 128], qbuf[h][0:96, sl])
        pt = C.pT1.next()
        P.act(pt[:, :, :], psS[:, :, :], AF.Exp, scale=SCALE1)
        return pt

    def back(grp, pt):
        h, qt, c, g0, gi, last_of_chunk = grp
        sl = slice(qt * 512, (qt + 1) * 512)
        psO = state[(h, qt)]
        for kb in range(2):
            P.mm(psO[:, :], C.VA1[c][:, g0 + kb, :], pt[:, kb, :],
                 start=(gi == 0 and kb == 0), stop=(gi == 32 and kb == 1))
        if last_of_chunk and qt == 3 and h + 1 < nheads:
            if c == 0:
                qbuf[h + 1] = QTr.next()
            l1_gen_kv(P, C, h + 1, c)
            if c < 4:
                l1_gen_q(P, C, h + 1, c, qbuf[h + 1])
        if gi == 32:
            p = h // 2
            if h % 2 == 0:
                lo, hi, slo, shi = 0, 64, 64, 128
            else:
                lo, hi, slo, shi = 64, 128, 0, 64
            rec = C.rec1.next()
            P.recip(rec[lo:hi, :], psO[slo:shi, :])
            o = C.rec1.next()
            P.tt("dve", o[lo:hi, :], psO[lo:hi, :], rec[lo:hi, :], ALU.mult)
            P.tt("pool", C.sgT[lo:hi, p, sl], o[lo:hi, :], C.sgT[lo:hi, p, sl], ALU.mult)

    prev = None
    for grp in groups:
        pt = front(grp)
        if prev is not None:
            back(*prev)
        prev = (grp, pt)
    if prev is not None:
        back(*prev)


def build_layer1(P, C, io):
    fused = io.mode == "fused"
    with Phase(P):
        alloc_l1_set(P, C)
        if not fused:
            P.dma("pool", C.ident[:, :], io.ident)
            P.memset("pool", C.ones32[:, :], 1.0)

        with Phase(P):
            C.w_in1 = P.sb([128, 8, 1440], BF16, "w_in1")
            C.psT = P.ps([128, 8, 128], BF16, "psT1")
            C.psK = P.ps([128, 2, 128], BF16, "psK1")
            C.hb = Ring([P.sb([128, D], BF16, "hb1") for _ in range(2)])
            C.hT = Ring([P.sb([128, 8, 128], BF16, "hT1") for _ in range(3)])
            C.ks = Ring([P.sb([128, 160], BF16, "ks") for _ in range(2)])
            C.cqn = Ring([P.sb([128, 256], BF16, "cqn") for _ in range(2)])
            C.qg = P.sb([128, 256], F32, "qg")
            C.kvg = P.sb([128, 128], F32, "kvg")
            cosB = P.sb([128, NB, 32], F32, "cosB")
            sinB = P.sb([128, NB, 32], F32, "sinB")
            kst_a = P.sb([128, TOK], BF16, "kst_a")
            kst_b = P.sb([32, TOK], BF16, "kst_b")
            xin = Ring([P.sb([128, D], F32, "xin1") for _ in range(3)])
            A1c = P.sb([128, D], F32, "A1c")
            B1c = P.sb([128, D], F32, "B1c")
            wv = io.w_in1.rearrange("(k p) n -> p k n", p=128)
            wloads = [(lambda k=k: P.dma("pool", C.w_in1[:, k, :], wv[:, k, :])) for k in range(8)]
            P.dma("sp", C.qg[:, :], io.qg.partition_broadcast(128))
            P.dma("sp", C.kvg[:, :], io.kvg.partition_broadcast(128))
            P.dma("sp", cosB[:, :, :], io.cosB.rearrange("(n p) d -> p n d", p=128))
            P.dma("sp", sinB[:, :, :], io.sinB.rearrange("(n p) d -> p n d", p=128))
            with Phase(P):
                alloc_ada_tmp(P, C)
                emit_silu_c(P, C, io.cT)
                emit_ada(P, C, io.ada_w1, io.ada_b1, io.ng1, (C.B_b, C.A_b, C.G_b), (B1c, A1c, None), extra=wloads)
            stop = getattr(io, "stop_after", None)
            if stop == "pre0":
                return
            ctx1v = io.ctx1_src.rearrange("(n p) d -> p n d", p=128)
            for cb in range(2):
                xt = xin.next()
                P.dma("sp", xt[:, :], ctx1v[:, cb, :])
                kb_tmp = C.kbt
                l1_block(P, C, xt[:, :], A1c, B1c, None, None, False,
                         C.ckvnT[:, SEQ + cb * 128:SEQ + (cb + 1) * 128], kb_tmp[0:32, :], None, None)
                P.copy("dve", C.KT1[16][64:96, cb * 128:(cb + 1) * 128], kb_tmp[0:32, :])
            if stop == "pre1":
                return
            x1v = io.x1_own.rearrange("(n p) d -> p n d", p=128)
            hTs = {}
            for i in range(NB + 1):
                if i < NB:
                    xt = xin.next()
                    P.dma("sp", xt[:, :], x1v[:, i, :])
                    hb = C.hb.next()
                    emit_modulate(P, C, xt[:, :], C.A_b, C.B_b, hb)
                    hTs[i] = C.hT.next()
                    emit_hT(P, C, hb, hTs[i])
                if i >= 1:
                    k = i - 1
                    tsl = slice(k * 128, (k + 1) * 128)
                    l1_block(P, C, None, None, None, cosB[:, k, :], sinB[:, k, :], True,
                             kst_a[:, tsl], kst_b[0:32, tsl], None, tsl, hT=hTs[k])
            if stop == "pre2":
                return
            gath = io.gath
            if fused:
                bnc = io.bounce
                P.dma("pool", bnc[0:128, :], kst_a[:, :])
                P.dma("pool", bnc[128:160, :], kst_b[0:32, :])
                P.all_gather(gath.tile, bnc.tile, [[0, 1, 2, 3], [4, 5, 6, 7]])
            else:
                cosF = P.sb([128, 4 * NB, 32], F32, "cosF")
                sinF = P.sb([128, 4 * NB, 32], F32, "sinF")
                cfv = io.cosF.rearrange("(n p) d -> p n d", p=128)
                sfv = io.sinF.rearrange("(n p) d -> p n d", p=128)
                for r in range(4):
                    P.dma("sp", cosF[:, r * NB:(r + 1) * NB, :], cfv[:, r * NB:(r + 1) * NB, :])
                    P.dma("sp", sinF[:, r * NB:(r + 1) * NB, :], sfv[:, r * NB:(r + 1) * NB, :])
                xfv = io.x1_full.rearrange("(n p) d -> p n d", p=128)
                for r in range(4):
                    for i in range(NB):
                        xt = xin.next()
                        P.dma("sp", xt[:, :], xfv[:, r * NB + i, :])
                        tsl = slice(i * 128, (i + 1) * 128)
                        l1_block(P, C, xt[:, :], C.A_b, C.B_b, cosF[:, r * NB + i, :], sinF[:, r * NB + i, :], False,
                                 kst_a[:, tsl], kst_b[0:32, tsl], None, None)
                    P.dma("pool", gath[r * 160:r * 160 + 128, :], kst_a[:, :])
                    P.dma("pool", gath[r * 160 + 128:r * 160 + 160, :], kst_b[0:32, :])
            if stop == "pre3":
                return
            for r in range(4):
                P.dma("sp", C.ckvnT[:, r * TOK:(r + 1) * TOK], gath[r * 160:r * 160 + 128, :])
                for cc in range(4):
                    P.dma("sp", C.KT1[r * 4 + cc][64:96, :], gath[r * 160 + 128:r * 160 + 160, cc * 512:(cc + 1) * 512])

        if getattr(io, "stop_after", None) == "pre":
            return
        with Phase(P):
            C.VA1 = [P.sb([128, 4, 128], BF16, "VA1") for _ in range(NCH)]
            C.psS = Ring([P.ps([128, 2, 512], F32, "psS1") for _ in range(2)])
            C.psO = Ring([P.ps([128, 512], F32, "psO1") for _ in range(2)])
            QTr = Ring([P.sb([96, TOK], BF16, "QTh") for _ in range(2)])
            C.cosT = P.sb([96, TOK], F32, "cosT")
            C.sinT = P.sb([96, TOK], F32, "sinT")
            C.pT1 = Ring([P.sb([128, 2, 512], BF16, "pT1") for _ in range(3)])
            C.w_uq = P.sb([128, 2, 1536], BF16, "w_uq")
            C.w_uqs = P.sb([128, 2, 1536], BF16, "w_uqs")
            C.w_ukv = P.sb([128, 2048], BF16, "w_ukv")
            C.rec1 = Ring([P.sb([128, 512], F32, "rec1") for _ in range(4)])
            C.rt = Ring([P.sb([96, 512], F32, "rt") for _ in range(4)])
            C.psG = C.psMM
            P.dma("pool", C.w_uq[:, :, :], io.w_uq.rearrange("(c p) n -> p c n", p=128))
            P.dma("pool", C.w_uqs[:, :, :], io.w_uqs.rearrange("(c p) n -> p c n", p=128))
            P.dma("pool", C.w_ukv[:, :], io.w_ukv)
            P.dma("sp", C.cosT[64:96, :], io.cosBT)
            P.dma("sp", C.sinT[64:96, :], io.sinBT)
            wov = io.w_o1.rearrange("(k p) n -> p k n", p=128)
            for k0 in range(0, 8, 4):
                P.dma("pool", C.w_o1[:, k0:k0 + 4, :], wov[:, k0:k0 + 4, :])
            P.dma("sp", C.fg[:, :], io.fg.partition_broadcast(128))
            l1_attn_all(P, C, io.nheads, QTr)

        if getattr(io, "stop_after", None) in ("genkv", "gen"):
            return
        with Phase(P):
            xin = Ring([P.sb([128, D], F32, "xin2") for _ in range(3)])
            xo = Ring([P.sb([128, D], F32, "xo") for _ in range(2)])
            yo = Ring([P.sb([128, D], F32, "yo") for _ in range(2)])
            x1v = io.x1_own.rearrange("(n p) d -> p n d", p=128)
            outv = io.out.rearrange("(n p) d -> p n d", p=128)
            xs = {}

            def load(i):
                xs[i] = xin.next()
                P.dma("sp", xs[i][:, :], x1v[:, i, :])

            load(0)
            for tb in range(NB):
                if tb + 1 < NB:
                    load(tb + 1)
                xt = xs[tb]
                xot = xo.next()
                for n in range(2):
                    ps = C.psMM.next()
                    for p in range(8):
                        P.mm(ps[:, :], C.sgT[:, p, tb * 128:(tb + 1) * 128], C.w_o1[:, p, n * 512:(n + 1) * 512],
                             start=(p == 0), stop=(p == 7))
                    t = C.t32.next()
                    sl = slice(n * 512, (n + 1) * 512)
                    P.tt("dve", t[:, 0:512], ps[:, :], C.G_b[:, sl], ALU.mult)
                    P.tt("pool", xot[:, sl], t[:, 0:512], xt[:, sl], ALU.add)
                ss = C.ss.next()
                rstd = C.rstd.next()
                emit_rms_rstd(P, C, xot[:, :], D, ss, rstd)
                yt = yo.next()
                P.stt("dve", yt[:, :], xot[:, :], rstd[:, 0:1], C.fg[:, :], ALU.mult, ALU.mult)
                P.dma("pool", outv[:, tb, :], yt[:, :], final=True)


def declare_inputs_b(nc, io):
    def din(name, shape, dt=F32):
        return nc.dram_tensor(name, list(shape), dt, kind="ExternalInput").ap()

    io.ada_w1 = din("ada_w1", [D, 3 * D])
    io.ada_b1 = din("ada_b1", [1, 3 * D])
    io.ng1 = din("ng1", [1, D])
    io.w_in1 = din("w_in1", [D, 1440])
    io.qg = din("qg", [1, 256])
    io.kvg = din("kvg", [1, 128])
    io.w_uq = din("w_uq", [256, 1536])
    io.w_uqs = din("w_uqs", [256, 1536])
    io.w_ukv = din("w_ukv", [128, 2048])
    io.w_o1 = din("w_o1", [D, D])
    io.fg = din("fg", [1, D])
    io.cosB = din("cosB", [TOK, 32])
    io.sinB = din("sinB", [TOK, 32])
    io.cosBT = din("cosBT", [32, TOK])
    io.sinBT = din("sinBT", [32, TOK])


def host_inputs_b(inp, c):
    b, j = c // 4, c % 4
    pos = np.arange(TOK * j, TOK * (j + 1))
    cosB, sinB = rope_tables(pos, 32)
    w_uq = inp["b_w_uq_1"]
    w3 = w_uq.reshape(256, 16, 96)
    sw = w3.copy()
    sw[:, :, 64:72], sw[:, :, 72:80] = w3[:, :, 72:80], w3[:, :, 64:72]
    sw[:, :, 80:88], sw[:, :, 88:96] = w3[:, :, 88:96], w3[:, :, 80:88]
    return {
        "ada_w1": inp["ada_w_1"], "ada_b1": inp["ada_b_1"].reshape(1, -1), "ng1": inp["norm_g_1"].reshape(1, -1),
        "w_in1": inp["b_w_in_1"], "qg": inp["b_q_norm_1"].reshape(1, -1), "kvg": inp["b_kv_norm_1"].reshape(1, -1),
        "w_uq": w_uq, "w_uqs": np.ascontiguousarray(sw.reshape(256, 1536)), "w_ukv": inp["b_w_ukv_1"],
        "w_o1": inp["b_w_o_1"], "fg": inp["final_g"].reshape(1, -1),
        "cosB": cosB, "sinB": sinB, "cosBT": np.ascontiguousarray(cosB.T), "sinBT": np.ascontiguousarray(sinB.T),
    }


NHEADS_DBG = 16
STOP_AFTER = None


def build_prog_b():
    nc = bass.Bass("TRN2", target_bir_lowering=False)
    io = Ctx()
    io.mode = "unfused"
    io.nheads = NHEADS_DBG
    io.stop_after = STOP_AFTER

    def din(name, shape, dt=F32):
        return nc.dram_tensor(name, list(shape), dt, kind="ExternalInput").ap()

    declare_inputs_b(nc, io)
    io.cT = din("cT", [128, 16])
    io.ident = din("ident", [128, 128])
    io.x1_own = din("x1_own", [TOK, D])
    io.x1_full = din("x1_full", [SEQ, D])
    io.ctx1_src = din("ctx1", [CTX, D])
    io.cosF = din("cosF", [SEQ, 32])
    io.sinF = din("sinF", [SEQ, 32])
    io.out = nc.dram_tensor("out", [TOK, D], F32, kind="ExternalOutput").ap()
    gath_h = nc.dram_tensor("gath", [640, TOK], BF16)
    with ExitStack() as st:
        P = Prog(nc, st)
        C = Ctx()
        alloc_globals(P, C)
        C.kbt = P.sb([32, 128], BF16, "kbt")
        gt = Tile(gath_h, "gath")
        io.gath = gt[:, :]
        build_layer1(P, C, io)
        P.flush(last=True)
        print("prog B: ninst", P.ninst, "nsem", P.nsem)
    return nc


def run_b(inp, x1, ctx1, trace=False):
    nc = build_prog_b()
    cosF, sinF = rope_tables(np.arange(SEQ), 32)
    in_maps = []
    for c in range(NCORE):
        b, j = c // 4, c % 4
        m = host_inputs_b(inp, c)
        a = host_inputs_a(inp, c)
        m["cT"] = a["cT"]
        m["ident"] = a["ident"]
        m["x1_own"] = np.ascontiguousarray(x1[b, j * TOK:(j + 1) * TOK])
        m["x1_full"] = np.ascontiguousarray(x1[b])
        m["ctx1"] = np.ascontiguousarray(ctx1[b])
        m["cosF"] = cosF
        m["sinF"] = sinF
        in_maps.append(m)
    return run_bass_kernel_spmd(nc, in_maps, core_ids=list(range(NCORE)), trace=trace)


def build_prog_fused():
    nc = bass.Bass("TRN2", target_bir_lowering=False)
    io = Ctx()
    declare_inputs_a(nc, io)
    declare_inputs_b(nc, io)
    io.out = nc.dram_tensor("out", [TOK, D], F32, kind="ExternalOutput").ap()
    x1s = nc.dram_tensor("x1s", [TOK, D], F32)
    ctx1s = nc.dram_tensor("ctx1s", [CTX, D], F32)
    bounce = nc.dram_tensor("bounce", [160, TOK], BF16)
    gath = nc.dram_tensor("gath", [640, TOK], BF16)
    with ExitStack() as st:
        P = Prog(nc, st)
        C = Ctx()
        alloc_globals(P, C)
        C.kbt = P.sb([32, 128], BF16, "kbt")
        x1t = Tile(x1s, "x1s")
        ctx1t = Tile(ctx1s, "ctx1s")
        io.x1 = x1t[:, :]
        io.ctx1 = ctx1t[:, :]
        io.final_x1 = False
        io.on_x1 = None
        io.on_ctx1 = None
        build_layer0(P, C, io)
        io.mode = "fused"
        io.nheads = NHEADS_DBG
        io.stop_after = STOP_AFTER
        io.x1_own = x1t[:, :]
        io.ctx1_src = ctx1t[:, :]
        io.gath = Tile(gath, "gath")[:, :]
        io.bounce = Tile(bounce, "bounce")[:, :]
        build_layer1(P, C, io)
        P.flush(last=True)
        print("prog fused: ninst", P.ninst, "nsem", P.nsem)
    return nc


def host_inputs_fused(inp, c):
    m = host_inputs_a(inp, c)
    m.update(host_inputs_b(inp, c))
    return m


def run_fused(inp, trace=False):
    nc = build_prog_fused()
    in_maps = [host_inputs_fused(inp, c) for c in range(NCORE)]
    return run_bass_kernel_spmd(nc, in_maps, core_ids=list(range(NCORE)), trace=trace)


FUSED = True


def kernel(**inputs):
    inp = {k: np.asarray(v) for k, v in inputs.items()}
    out = np.zeros((2, SEQ, D), np.float32)
    if FUSED:
        res = run_fused(inp)
    else:
        res = run_a(inp)
        x1 = np.zeros((2, SEQ, D), np.float32)
        ctx1 = np.zeros((2, CTX, D), np.float32)
        for c in range(NCORE):
            b, j = c // 4, c % 4
            x1[b, j * TOK:(j + 1) * TOK] = res.results[c]["x1"]
            ctx1[b] = res.results[c]["ctx1"]
        res = run_b(inp, x1, ctx1)
    for c in range(NCORE):
        b, j = c // 4, c % 4
        out[b, j * TOK:(j + 1) * TOK] = res.results[c]["out"]
    return out
```

```python
import numpy as np
import ml_dtypes
from contextlib import ExitStack
import concourse.bass as bass
import concourse.mybir as mybir
from concourse.bass_utils import run_bass_kernel_spmd

F32 = mybir.dt.float32
BF16 = mybir.dt.bfloat16
AF = mybir.ActivationFunctionType
ALU = mybir.AluOpType

ENGS = ("pe", "act", "dve", "pool", "sp")
SELF_SYNC = True


class Tile:
    def __init__(self, handle, name):
        self.h = handle
        self.name = name
        self.w = None
        self.dma_w = 0
        self.reads = {}
        self.pend_r = []
        self.dsem = None
        self.dcnt = 0

    def __getitem__(self, key):
        return View(self, self.h[key])


class View:
    def __init__(self, tile, ap):
        self.tile = tile
        self.ap = ap

    def __getitem__(self, key):
        return View(self.tile, self.ap[key])

    def rearrange(self, s, **kw):
        return View(self.tile, self.ap.rearrange(s, **kw))

    def unsqueeze(self, ax):
        return View(self.tile, self.ap.unsqueeze(ax))

    def to_broadcast(self, shape):
        return View(self.tile, self.ap.to_broadcast(list(shape)))


class Prog:
    def __init__(self, nc, stack):
        self.nc = nc
        self.stack = stack
        self.gstack = stack
        self.q = {e: [] for e in ENGS}
        self.cnt = {e: 0 for e in ENGS}
        self.seen = {e: {} for e in ENGS}
        self.sem = {e: stack.enter_context(nc.semaphore("s_" + e)) for e in ENGS}
        self.nsem = len(ENGS)
        self.ninst = 0
        self.final_dma = []
        self.uid = 0
        self.dma_tiles = []
        self.bar_tile = Tile(stack.enter_context(nc.sbuf_tensor("bar_t", [128, 1], F32)), "bar_t")

    def sb(self, shape, dt, name):
        self.uid += 1
        nm = "%s_%d" % (name, self.uid)
        return Tile(self.stack.enter_context(self.nc.sbuf_tensor(nm, list(shape), dt)), nm)

    def ps(self, shape, dt, name):
        self.uid += 1
        nm = "%s_%d" % (name, self.uid)
        return Tile(self.stack.enter_context(self.nc.psum_tensor(nm, list(shape), dt)), nm)

    def dram(self, handle, name):
        return Tile(handle, name)

    def _dsem(self, t):
        if t.dsem is None:
            t.dsem = self.gstack.enter_context(self.nc.semaphore("d_" + t.name))
            self.nsem += 1
            self.dma_tiles.append(t)
        return t.dsem

    def _wait(self, e, key, sem, val):
        if val <= 0 or self.seen[e].get(key, 0) >= val:
            return
        self.seen[e][key] = val
        self.q[e].append(("wait", sem, val))

    def _deps(self, e, reads, writes):
        for t in reads:
            if t.w is not None:
                we, wc = t.w
                if we != e or SELF_SYNC:
                    self._wait(e, we, self.sem[we], wc)
            if t.dma_w:
                self._wait(e, id(t), t.dsem, t.dma_w)
        for t in writes:
            if t.w is not None:
                we, wc = t.w
                if we != e:
                    self._wait(e, we, self.sem[we], wc)
            for re_, rc in t.reads.items():
                if re_ != e or SELF_SYNC:
                    self._wait(e, re_, self.sem[re_], rc)
            if t.dma_w:
                self._wait(e, id(t), t.dsem, t.dma_w)
            for key, sem, val in t.pend_r:
                self._wait(e, key, sem, val)

    @staticmethod
    def _tiles(vs):
        out = []
        for v in vs:
            if v is None:
                continue
            t = v.tile if isinstance(v, View) else v
            if t not in out:
                out.append(t)
        return out

    def op(self, e, fn, reads, writes):
        reads = self._tiles(reads)
        writes = self._tiles(writes)
        self._deps(e, reads, writes)
        self.cnt[e] += 1
        c = self.cnt[e]
        self.q[e].append(("op", fn))
        for t in reads:
            t.reads[e] = c
        for t in writes:
            t.w = (e, c)
            t.reads = {}
            t.dma_w = 0
            t.pend_r = []
        self.ninst += 1

    def dma(self, e, out, in_, final=False):
        o_ap = out.ap if isinstance(out, View) else out
        i_ap = in_.ap if isinstance(in_, View) else in_
        ot = out.tile if isinstance(out, View) else None
        it = in_.tile if isinstance(in_, View) else None
        self._deps(e, [it] if it is not None else [], [ot] if ot is not None else [])
        t = ot if ot is not None else it
        sem = self._dsem(t)
        t.dcnt += 16
        val = t.dcnt
        self.q[e].append(("dma", o_ap, i_ap, [sem]))
        if ot is not None:
            ot.w = None
            ot.reads = {}
            ot.dma_w = val
            ot.pend_r = []
        if it is not None:
            it.pend_r.append((id(t), sem, val))
        if final:
            self.final_dma.append((sem, val, id(t)))
        self.ninst += 1

    def all_gather(self, out_tile, in_tile, groups):
        e = "pool"
        self._deps(e, [in_tile], [out_tile])
        sem = self.gstack.enter_context(self.nc.semaphore("cc_sem"))
        self.nsem += 1
        out_tile.dsem = sem
        out_tile.dcnt = 1
        self.dma_tiles.append(out_tile)

        def fn(eng):
            eng.collective_compute("AllGather", ALU.bypass, replica_groups=groups,
                                   ins=[in_tile.h.ap()], outs=[out_tile.h.ap()]).then_inc(sem)

        self.q[e].append(("raw", fn))
        out_tile.w = None
        out_tile.reads = {}
        out_tile.dma_w = 1
        out_tile.pend_r = []
        in_tile.pend_r.append((id(out_tile), sem, 1))
        self.ninst += 1

    def barrier(self):
        for f in ENGS:
            if f != "dve":
                self._wait("dve", f, self.sem[f], self.cnt[f])
        for t in self.dma_tiles:
            if t.dcnt:
                self._wait("dve", id(t), t.dsem, t.dcnt)
        self.memset("dve", self.bar_tile[:, :], 0.0)
        c = self.cnt["dve"]
        for e in ENGS:
            if e != "dve":
                self._wait(e, "dve", self.sem["dve"], c)

    def flush(self, last=False):
        nc = self.nc
        if last:
            for sem, val, key in self.final_dma:
                self._wait("sp", key, sem, val)
        q = self.q
        sems = self.sem

        def run(engname, eng):
            mysem = sems[engname]
            for item in q[engname]:
                if item[0] == "wait":
                    eng.wait_ge(item[1], item[2])
                elif item[0] == "op":
                    item[1](eng).then_inc(mysem, 1)
                elif item[0] == "dma":
                    _, o_ap, i_ap, sl = item
                    ins = eng.dma_start(out=o_ap, in_=i_ap)
                    for s in sl:
                        ins = ins.then_inc(s, 16)
                else:
                    item[1](eng)

        with nc.Block() as block:
            @block.tensor
            def _(eng):
                run("pe", eng)

            @block.scalar
            def _(eng):
                run("act", eng)

            @block.vector
            def _(eng):
                run("dve", eng)

            @block.gpsimd
            def _(eng):
                run("pool", eng)

            @block.sync
            def _(eng):
                run("sp", eng)
        self.q = {e: [] for e in ENGS}

    def mm(self, out, lhsT, rhs, start=True, stop=True):
        self.op("pe", lambda eng: eng.matmul(out.ap, lhsT.ap, rhs.ap, start=start, stop=stop),
                [lhsT, rhs], [out])

    def transpose(self, out, in_, ident):
        self.op("pe", lambda eng: eng.transpose(out.ap, in_.ap, ident.ap), [in_, ident], [out])

    def act(self, out, in_, func, bias=None, scale=None, accum_out=None):
        kw = {}
        reads = [in_]
        writes = [out]
        if bias is not None:
            kw["bias"] = bias.ap if isinstance(bias, View) else bias
            if isinstance(bias, View):
                reads.append(bias)
        if scale is not None:
            kw["scale"] = scale.ap if isinstance(scale, View) else scale
            if isinstance(scale, View):
                reads.append(scale)
        if accum_out is not None:
            kw["accum_out"] = accum_out.ap
            writes.append(accum_out)
        self.op("act", lambda e: e.activation(out.ap, in_.ap, func, **kw), reads, writes)

    def copy(self, e, out, in_):
        if e == "act":
            self.op(e, lambda eng: eng.copy(out.ap, in_.ap), [in_], [out])
        else:
            self.op(e, lambda eng: eng.tensor_copy(out.ap, in_.ap), [in_], [out])

    def tt(self, e, out, in0, in1, op):
        self.op(e, lambda eng: eng.tensor_tensor(out.ap, in0.ap, in1.ap, op), [in0, in1], [out])

    def ts(self, e, out, in0, s1, s2, op0, op1=None):
        reads = [in0]
        a1 = s1.ap if isinstance(s1, View) else s1
        a2 = s2.ap if isinstance(s2, View) else s2
        if isinstance(s1, View):
            reads.append(s1)
        if isinstance(s2, View):
            reads.append(s2)
        kw = {}
        if op1 is not None:
            kw["op1"] = op1
        self.op(e, lambda eng: eng.tensor_scalar(out.ap, in0.ap, a1, a2, op0, **kw), reads, [out])

    def stt(self, e, out, in0, scalar, in1, op0, op1):
        reads = [in0, in1]
        a = scalar.ap if isinstance(scalar, View) else scalar
        if isinstance(scalar, View):
            reads.append(scalar)
        self.op(e, lambda eng: eng.scalar_tensor_tensor(out.ap, in0.ap, a, in1.ap, op0, op1), reads, [out])

    def memset(self, e, out, val):
        self.op(e, lambda eng: eng.memset(out.ap, val), [], [out])

    def recip(self, out, in_):
        self.op("dve", lambda eng: eng.reciprocal(out.ap, in_.ap), [in_], [out])


class Ring:
    def __init__(self, tiles):
        self.tiles = tiles
        self.i = 0

    def next(self):
        t = self.tiles[self.i % len(self.tiles)]
        self.i += 1
        return t


D = 1024
SEQ = 8192
NCORE = 8
TOK = 2048
NB = TOK // 128
CTX = 256
EPS = 1e-6
GRID_W = 64
ROPE_BASE = 10000.0


def rope_tables(pos, rot_dim):
    pos = np.asarray(pos)
    row = (pos // GRID_W).astype(np.float32)
    col = (pos % GRID_W).astype(np.float32)
    nf = rot_dim // 4
    inv = (np.float32(ROPE_BASE) ** (-np.arange(nf, dtype=np.float32) / np.float32(nf))).astype(np.float32)
    ar = row[:, None] * inv
    ac = col[:, None] * inv
    ang = np.concatenate([ar, ar, ac, ac], axis=-1).astype(np.float32)
    cos = np.cos(ang).astype(np.float32)
    sin = np.sin(ang).astype(np.float32)
    sgn = np.concatenate([-np.ones(nf), np.ones(nf), -np.ones(nf), np.ones(nf)]).astype(np.float32)
    return cos, (sin * sgn).astype(np.float32)


class Ctx:
    pass


class Phase:
    def __init__(self, P):
        self.P = P

    def __enter__(self):
        self.prev = self.P.stack
        self.st = ExitStack()
        self.st.__enter__()
        self.P.stack = self.st
        return self

    def __exit__(self, *a):
        self.P.barrier()
        self.P.flush()
        self.P.stack = self.prev
        return self.st.__exit__(*a)


def emit_rms_rstd(P, C, x_view, n, ss, rstd):
    P.memset("pool", ss[:, 0:1], 0.0)
    P.act(C.junk[:, 0:n], x_view, AF.Square, accum_out=ss[:, 0:1])
    P.act(rstd[:, 0:1], ss[:, 0:1], AF.Ln, bias=C.eps_t[:, 0:1], scale=1.0 / n)
    P.act(rstd[:, 0:1], rstd[:, 0:1], AF.Exp, scale=-0.5)


def emit_modulate(P, C, x_view, A, B, hb):
    ss = C.ss.next()
    rstd = C.rstd.next()
    emit_rms_rstd(P, C, x_view, D, ss, rstd)
    t = C.t32.next()
    P.stt("dve", t[:, :], x_view, rstd[:, 0:1], A[:, :], ALU.mult, ALU.mult)
    P.tt("dve", hb[:, :], t[:, :], B[:, :], ALU.add)


def emit_hT(P, C, hb, hT):
    ps = C.psT
    for k in range(8):
        P.transpose(ps[:, k, :], hb[:, k * 128:(k + 1) * 128], C.ident[:, :])
    P.copy("dve", hT[:, :, :], ps[:, :, :])


def emit_proj(P, hT, w, c0, c1, ps):
    for k in range(8):
        P.mm(ps[:, 0:c1 - c0], hT[:, k, :], w[:, k, c0:c1], start=(k == 0), stop=(k == 7))


def emit_rope_tm(P, C, src, nh, hd, cos_v, sin_v, dst):
    q = hd // 4
    t1 = C.t32.next()
    t2 = C.t32.next()
    n = nh * hd
    s3 = src.rearrange("p (h d) -> p h d", h=nh)
    P.tt("dve", t1[:, 0:n].rearrange("p (h d) -> p h d", h=nh), s3,
         cos_v.unsqueeze(1).to_broadcast([128, nh, hd]), ALU.mult)
    s5 = src.rearrange("p (h a b d) -> p h a b d", h=nh, a=2, b=2, d=q)
    t5 = t2[:, 0:n].rearrange("p (h a b d) -> p h a b d", h=nh, a=2, b=2, d=q)
    sn = sin_v.rearrange("p (a b d) -> p a b d", a=2, b=2, d=q)
    for half in range(2):
        P.tt("dve" if half == 1 else "pool", t5[:, :, :, half, :], s5[:, :, :, 1 - half, :],
             sn[:, :, half, :].unsqueeze(1).to_broadcast([128, nh, 2, q]), ALU.mult)
    P.tt("dve", dst, t1[:, 0:n], t2[:, 0:n], ALU.add)


def emit_head_transposes(P, C, src_bf, nheads, dst):
    for g0 in range(0, nheads, 8):
        ps = C.psH
        ng = min(8, nheads - g0)
        for g in range(ng):
            h = g0 + g
            P.transpose(ps[0:64, g, :], src_bf[:, h * 64:(h + 1) * 64], C.ident[:, :])
        P.copy("dve", dst[:, g0:g0 + ng, :], ps[0:64, 0:ng, :])


def emit_silu_from_psum(P, C, ps_view, n, dst_bf):
    u = C.t32.next()
    P.act(u[:, 0:n], ps_view, AF.Exp, scale=-1.0)
    P.act(u[:, 0:n], u[:, 0:n], AF.Ln, bias=C.one_t[:, 0:1])
    P.act(u[:, 0:n], u[:, 0:n], AF.Exp, scale=-1.0)
    P.tt("dve", dst_bf, ps_view, u[:, 0:n], ALU.mult)


def alloc_ada_tmp(P, C):
    C.cT = P.sb([128, 16], F32, "cT")
    C.cu = P.sb([128, 16], F32, "cu")
    C.cxh = P.sb([128, 16], F32, "cxh")
    C.lhs_b = P.sb([128, 8, 128], BF16, "lhs_b")
    C.lhs_c = P.sb([128, 8, 128], BF16, "lhs_c")
    C.ada_bias = Ring([P.sb([128, 256], F32, "ada_bias") for _ in range(2)])
    C.ng_bc = P.sb([128, D], F32, "ng_bc")
    C.adaw = Ring([P.sb([128, 8, 256], BF16, "adaw") for _ in range(1)])


def emit_silu_c(P, C, cT_ap):
    ct = C.cT
    P.dma("sp", ct[:, :], cT_ap)
    u = C.cu
    xh = C.cxh
    P.act(u[:, :], ct[:, :], AF.Exp, scale=-1.0)
    P.act(u[:, :], u[:, :], AF.Ln, bias=C.one_t[:, 0:1])
    P.act(u[:, :], u[:, :], AF.Exp, scale=-1.0)
    P.tt("dve", u[:, :], u[:, :], ct[:, :], ALU.mult)
    for r, lhs in ((0, C.lhs_b), (1, C.lhs_c)):
        for k in range(8):
            P.ts("dve", lhs[:, k, :], C.ones32[:, :], u[:, r * 8 + k:r * 8 + k + 1], None, ALU.mult)


def emit_ada(P, C, ada_w_ap, ada_b_ap, ng_ap, outs_b, outs_c):
    ng = C.ng_bc
    P.dma("sp", ng[:, :], ng_ap.partition_broadcast(128))
    wv = ada_w_ap.rearrange("(k p) n -> p k n", p=128)
    NT = 256
    for n in range(3 * D // NT):
        wt = C.adaw.next()
        P.dma("pool", wt[:, :, :], wv[:, :, n * NT:(n + 1) * NT])
        bias = C.ada_bias.next()
        P.dma("sp", bias[:, :], ada_b_ap[:, n * NT:(n + 1) * NT].partition_broadcast(128))
        kind = (n * NT) // D
        sl = slice((n * NT) % D, (n * NT) % D + NT)
        for lhs, outs in ((C.lhs_b, outs_b), (C.lhs_c, outs_c)):
            if outs is None or outs[kind] is None:
                continue
            ps = C.psMM.next()
            for k in range(8):
                P.mm(ps[:, 0:NT], lhs[:, k, :], wt[:, k, :], start=(k == 0), stop=(k == 7))
            dst = outs[kind]
            if kind == 1:
                t = C.t32.next()
                P.tt("dve", t[:, 0:NT], ps[:, 0:NT], bias[:, :], ALU.add)
                P.stt("dve", dst[:, sl], t[:, 0:NT], 1.0, ng[:, sl], ALU.add, ALU.mult)
            else:
                P.tt("dve", dst[:, sl], ps[:, 0:NT], bias[:, :], ALU.add)


def emit_attn_a(P, C, QT, sgT, keyblocks, ogT, only_kv=None):
    for kv in (range(4) if only_kv is None else [only_kv]):
        q_rhs = QT[:, kv * 4:(kv + 1) * 4, :].rearrange("p g q -> p (g q)")
        psO = C.psO.next()
        nk = len(keyblocks)

        def front(bi):
            KT, VA, mask = keyblocks[bi]
            psS = C.psS.next()
            P.mm(psS[:, :], KT[:, kv, :], q_rhs)
            pt = C.pT.next()
            P.act(pt[:, :], psS[:, :], AF.Exp, scale=0.125)
            if mask is not None:
                P.tt("dve", pt[:, :].rearrange("p (g q) -> p g q", g=4),
                     pt[:, :].rearrange("p (g q) -> p g q", g=4),
                     mask.unsqueeze(1).to_broadcast([128, 4, 128]), ALU.mult)
            return pt

        def back(bi, pt):
            KT, VA, mask = keyblocks[bi]
            P.mm(psO[:, :], VA[:, kv, :], pt[:, :], start=(bi == 0), stop=False)

        prev = None
        for bi in range(nk):
            pt = front(bi)
            if prev is not None:
                back(*prev)
            prev = (bi, pt)
        back(*prev)
        P.mm(psO[:, :], C.sel[0:1, :], C.esink[0:1, kv * 512:(kv + 1) * 512], start=False, stop=True)
        rec = C.rec.next()
        P.act(rec[0:64, :], psO[64:128, :], AF.Ln)
        P.act(rec[0:64, :], rec[0:64, :], AF.Exp, scale=-1.0)
        o = C.o32.next()
        P.tt("dve", o[0:64, :], psO[0:64, :], rec[0:64, :], ALU.mult)
        o4 = o[0:64, :].rearrange("p (c e q) -> p c e q", c=2, e=2)
        s4 = sgT[:, kv * 4:(kv + 1) * 4, :].rearrange("p (c e) q -> p c e q", c=2)
        for par in range(2):
            P.tt("dve" if par == 0 else "pool", ogT[par * 64:(par + 1) * 64, kv * 2:kv * 2 + 2, :],
                 o4[:, :, par, :], s4[:, :, par, :], ALU.mult)


def emit_out_a(P, C, ogT, w_o, G, x_tile, x1):
    for n in range(2):
        ps = C.psO.next()
        for k in range(8):
            P.mm(ps[:, :], ogT[:, k, :], w_o[:, k, n * 512:(n + 1) * 512], start=(k == 0), stop=(k == 7))
        t = C.ty.next()
        sl = slice(n * 512, (n + 1) * 512)
        P.tt("dve", t[:, 0:512], ps[:, :], G[:, sl], ALU.mult)
        P.tt("pool", x1[:, sl], t[:, 0:512], x_tile[:, sl], ALU.add)


def l0_part1a(P, C, x_view, A, B):
    hb = C.hb.next()
    emit_modulate(P, C, x_view, A, B, hb)
    hT = C.hT.next()
    emit_hT(P, C, hb, hT)
    return hT


def l0_part1b(P, C, hT, cos_v, sin_v, KT, VA):
    ps = C.psMM.next()
    emit_proj(P, hT, C.w_in0, 1024, 1536, ps)
    kr = C.kr.next()
    if cos_v is not None:
        k32 = C.t32.next()
        P.copy("act", k32[:, 0:256], ps[:, 0:256])
        emit_rope_tm(P, C, k32[:, 0:256], 4, 64, cos_v, sin_v, kr[:, :])
    else:
        P.copy("act", kr[:, :], ps[:, 0:256])
    P.copy("act", VA[:, :, 0:64], ps[:, 256:512].rearrange("p (h d) -> p h d", h=4))
    P.memset("pool", VA[:, :, 64:128], 1.0)
    emit_head_transposes(P, C, kr, 4, KT)


def l0_part1(P, C, x_view, A, B, cos_v, sin_v, KT, VA):
    hT = l0_part1a(P, C, x_view, A, B)
    l0_part1b(P, C, hT, cos_v, sin_v, KT, VA)
    return hT


def l0_part2q(P, C, hT, cos_v, sin_v, QT):
    q32 = C.q32
    for n in range(2):
        ps = C.psMM.next()
        emit_proj(P, hT, C.w_in0, n * 512, (n + 1) * 512, ps)
        P.copy("act", q32[:, n * 512:(n + 1) * 512], ps[:, :])
    qr = C.qr
    if cos_v is not None:
        emit_rope_tm(P, C, q32[:, :], 16, 64, cos_v, sin_v, qr[:, :])
    else:
        P.copy("pool", qr[:, :], q32[:, :])
    emit_head_transposes(P, C, qr, 16, QT)


def l0_part2g(P, C, hT, sgT):
    sg = C.sg
    for n in range(2):
        ps = C.psMM.next()
        emit_proj(P, hT, C.w_in0, 1536 + n * 512, 1536 + (n + 1) * 512, ps)
        emit_silu_from_psum(P, C, ps[:, :], 512, sg[:, n * 512:(n + 1) * 512])
    emit_head_transposes(P, C, sg, 16, sgT)


def l0_part2(P, C, hT, cos_v, sin_v, QT, sgT):
    l0_part2q(P, C, hT, cos_v, sin_v, QT)
    l0_part2g(P, C, hT, sgT)


def alloc_l0_set(P, C):
    C.w_in0 = P.sb([128, 8, 2560], BF16, "w_in0")
    C.w_o0 = P.sb([128, 8, 1024], BF16, "w_o0")
    C.masks = P.sb([128, 512], BF16, "masks")
    C.cosA = P.sb([128, NB + 2, 64], F32, "cosA")
    C.sinA = P.sb([128, NB + 2, 64], F32, "sinA")
    C.sel = P.sb([1, 128], BF16, "sel")
    C.sink32 = P.sb([1, 16], F32, "sink32")
    C.esink = P.sb([1, 2048], BF16, "esink")
    C.hb = Ring([P.sb([128, D], BF16, "hb") for _ in range(2)])
    C.hT = Ring([P.sb([128, 8, 128], BF16, "hT") for _ in range(3)])
    C.ty = Ring([P.sb([128, 512], F32, "ty") for _ in range(2)])
    C.kr = Ring([P.sb([128, 256], BF16, "kr") for _ in range(2)])
    C.q32 = P.sb([128, D], F32, "q32")
    C.qr = P.sb([128, D], BF16, "qr")
    C.sg = P.sb([128, D], BF16, "sg")
    C.QT = Ring([P.sb([64, 16, 128], BF16, "QT") for _ in range(3)])
    C.sgT = Ring([P.sb([64, 16, 128], BF16, "sgT") for _ in range(3)])
    C.KTc = [P.sb([64, 4, 128], BF16, "KTc") for _ in range(2)]
    C.VAc = [P.sb([128, 4, 128], BF16, "VAc") for _ in range(2)]
    C.pT = Ring([P.sb([128, 512], BF16, "pT") for _ in range(3)])
    C.rec = Ring([P.sb([64, 512], F32, "rec") for _ in range(2)])
    C.o32 = Ring([P.sb([64, 512], F32, "o32") for _ in range(1)])
    C.ogT = Ring([P.sb([128, 8, 128], BF16, "ogT") for _ in range(2)])
    C.x1 = Ring([P.sb([128, D], F32, "x1") for _ in range(2)])
    C.psT = P.ps([128, 8, 128], BF16, "psT")
    C.psH = P.ps([128, 8, 128], BF16, "psH")
    C.psMM = Ring([P.ps([128, 512], F32, "psMM") for _ in range(2)])
    C.psS = Ring([P.ps([128, 512], F32, "psS") for _ in range(2)])
    C.psO = Ring([P.ps([128, 512], F32, "psO") for _ in range(2)])


def alloc_globals(P, C):
    C.ident = P.sb([128, 128], BF16, "ident")
    C.ones32 = P.sb([128, 128], F32, "ones32")
    C.junk = P.sb([128, D], BF16, "junk")
    C.ss = Ring([P.sb([128, 1], F32, "ss") for _ in range(4)])
    C.rstd = Ring([P.sb([128, 1], F32, "rstd") for _ in range(4)])
    C.t32 = Ring([P.sb([128, D], F32, "t32") for _ in range(3)])
    C.eps_t = P.sb([128, 1], F32, "eps_t")
    P.memset("pool", C.eps_t[:, :], EPS)
    C.one_t = P.sb([128, 1], F32, "one_t")
    P.memset("pool", C.one_t[:, :], 1.0)
    C.A_b = P.sb([128, D], F32, "A_b")
    C.B_b = P.sb([128, D], F32, "B_b")
    C.G_b = P.sb([128, D], F32, "G_b")


def build_layer0(P, C, io):
    with Phase(P):
        alloc_l0_set(P, C)
        w_in = C.w_in0
        w_o = C.w_o0
        wv = io.w_in0.rearrange("(k p) n -> p k n", p=128)
        for k in range(8):
            P.dma("pool", w_in[:, k, :], wv[:, k, :])
        wov = io.w_o0.rearrange("(k p) n -> p k n", p=128)
        for k0 in range(0, 8, 4):
            P.dma("pool", w_o[:, k0:k0 + 4, :], wov[:, k0:k0 + 4, :])
        P.dma("pool", C.ident[:, :], io.ident)
        P.dma("pool", C.masks[:, :], io.masks)
        P.dma("sp", C.cosA[:, :, :], io.cosA.rearrange("(n p) d -> p n d", p=128))
        P.dma("sp", C.sinA[:, :, :], io.sinA.rearrange("(n p) d -> p n d", p=128))
        P.memset("pool", C.ones32[:, :], 1.0)
        P.memset("pool", C.sel[:, 0:64], 0.0)
        P.memset("pool", C.sel[:, 64:128], 1.0)
        P.dma("sp", C.sink32[0:1, :], io.sinks)
        P.act(C.sink32[0:1, :], C.sink32[0:1, :], AF.Exp)
        P.copy("dve", C.esink[0:1, :].rearrange("p (h q) -> p h q", h=16),
               C.sink32[0:1, :].unsqueeze(2).to_broadcast([1, 16, 128]))

        ctx_keys = [(C.KTc[0], C.VAc[0], None), (C.KTc[1], C.VAc[1], None)]
        with Phase(P):
            alloc_ada_tmp(P, C)
            A_c = P.sb([128, D], F32, "A_c")
            B_c = P.sb([128, D], F32, "B_c")
            G_c = P.sb([128, D], F32, "G_c")
            cxt = [P.sb([128, D], F32, "cx") for _ in range(2)]
            emit_silu_c(P, C, io.cT)
            emit_ada(P, C, io.ada_w0, io.ada_b0, io.ng0, (C.B_b, C.A_b, C.G_b), (B_c, A_c, G_c))
            ctxv = io.ctx.rearrange("(n p) d -> p n d", p=128)
            hTs = []
            for cb in range(2):
                P.dma("sp", cxt[cb][:, :], ctxv[:, cb, :])
                hTs.append(l0_part1(P, C, cxt[cb][:, :], A_c, B_c, None, None, C.KTc[cb], C.VAc[cb]))
            ctx1v = io.ctx1.rearrange("(n p) d -> p n d", p=128)
            for cb in range(2):
                QT = C.QT.next()
                sgT = C.sgT.next()
                l0_part2(P, C, hTs[cb], None, None, QT, sgT)
                ogT = C.ogT.next()
                emit_attn_a(P, C, QT, sgT, ctx_keys, ogT)
                x1 = C.x1.next()
                emit_out_a(P, C, ogT, w_o, G_c, cxt[cb], x1)
                P.dma("pool", ctx1v[:, cb, :], x1[:, :], final=io.final_x1)
                if io.on_ctx1 is not None:
                    io.on_ctx1(cb, x1)

        with Phase(P):
            xin = Ring([P.sb([128, D], F32, "xin") for _ in range(6)])
            KTr = Ring([P.sb([64, 4, 128], BF16, "KT") for _ in range(5)])
            VAr = Ring([P.sb([128, 4, 128], BF16, "VA") for _ in range(5)])
            xv = io.xh.rearrange("(n p) d -> p n d", p=128)
            x1v = io.x1.rearrange("(n p) d -> p n d", p=128)
            xs, KTs, VAs, QTs, sgTs = {}, {}, {}, {}, {}

            def load(i):
                xs[i] = xin.next()
                P.dma("sp", xs[i][:, :], xv[:, i, :])

            def attn_chunk(j, kv, ogTs):
                mprev = C.masks[:, 256:384] if j == 1 else C.masks[:, 0:128]
                mnext = C.masks[:, 384:512] if j == NB else C.masks[:, 128:256]
                keys = [(KTs[j - 1], VAs[j - 1], mprev), (KTs[j], VAs[j], None),
                        (KTs[j + 1], VAs[j + 1], mnext)] + ctx_keys
                if kv == 0:
                    ogTs[j] = C.ogT.next()
                emit_attn_a(P, C, QTs[j], sgTs[j], keys, ogTs[j], only_kv=kv)

            def out_chunk(j, ogTs):
                x1 = C.x1.next()
                emit_out_a(P, C, ogTs[j], w_o, C.G_b, xs[j], x1)
                P.dma("pool", x1v[:, j - 1, :], x1[:, :], final=io.final_x1)

            ogTs = {}
            load(0)
            load(1)
            for i in range(NB + 4):
                j = i - 2
                doj = 1 <= j <= NB
                dox = i < NB + 2
                full = 1 <= i <= NB
                if i + 2 < NB + 2:
                    load(i + 2)
                if dox:
                    KTs[i] = KTr.next()
                    VAs[i] = VAr.next()
                    cv, sv = C.cosA[:, i, :], C.sinA[:, i, :]
                    hT = l0_part1a(P, C, xs[i][:, :], C.A_b, C.B_b)
                if doj:
                    attn_chunk(j, 0, ogTs)
                if dox:
                    l0_part1b(P, C, hT, cv, sv, KTs[i], VAs[i])
                if doj:
                    attn_chunk(j, 1, ogTs)
                if dox and full:
                    QTs[i] = C.QT.next()
                    l0_part2q(P, C, hT, cv, sv, QTs[i])
                if doj:
                    attn_chunk(j, 2, ogTs)
                if dox and full:
                    sgTs[i] = C.sgT.next()
                    l0_part2g(P, C, hT, sgTs[i])
                if doj:
                    attn_chunk(j, 3, ogTs)
                    out_chunk(j, ogTs)


def declare_inputs_a(nc, io):
    def din(name, shape):
        return nc.dram_tensor(name, list(shape), F32, kind="ExternalInput").ap()

    io.xh = din("xh", [(NB + 2) * 128, D])
    io.ctx = din("ctx", [CTX, D])
    io.cT = din("cT", [128, 16])
    io.ada_w0 = din("ada_w0", [D, 3 * D])
    io.ada_b0 = din("ada_b0", [1, 3 * D])
    io.ng0 = din("ng0", [1, D])
    io.w_in0 = din("w_in0", [D, 2560])
    io.w_o0 = din("w_o0", [D, D])
    io.sinks = din("sinks", [1, 16])
    io.cosA = din("cosA", [(NB + 2) * 128, 64])
    io.sinA = din("sinA", [(NB + 2) * 128, 64])
    io.masks = din("masks", [128, 512])
    io.ident = din("ident", [128, 128])


def build_prog_a():
    nc = bass.Bass("TRN2", target_bir_lowering=False)
    io = Ctx()
    declare_inputs_a(nc, io)
    io.x1 = nc.dram_tensor("x1", [TOK, D], F32, kind="ExternalOutput").ap()
    io.ctx1 = nc.dram_tensor("ctx1", [CTX, D], F32, kind="ExternalOutput").ap()
    io.final_x1 = True
    io.on_x1 = None
    io.on_ctx1 = None
    with ExitStack() as st:
        P = Prog(nc, st)
        C = Ctx()
        alloc_globals(P, C)
        build_layer0(P, C, io)
        P.flush(last=True)
        print("prog A: ninst", P.ninst, "nsem", P.nsem)
    return nc


def host_inputs_a(inp, c):
    b, j = c // 4, c % 4
    t0 = TOK * j
    x = inp["x"]
    xh = np.zeros(((NB + 2) * 128, D), np.float32)
    lo, hi = t0 - 128, t0 + TOK + 128
    slo, shi = max(lo, 0), min(hi, SEQ)
    xh[slo - lo:shi - lo] = x[b, slo:shi]
    pos = np.clip(np.arange(lo, hi), 0, SEQ - 1)
    cosA, sinA = rope_tables(pos, 64)
    cT = np.zeros((128, 16), np.float32)
    cT[:, 0:8] = inp["c"][b].reshape(8, 128).T
    cT[:, 8:16] = inp["c_ctx"].reshape(8, 128).T
    kk = np.arange(128)[:, None]
    qq = np.arange(128)[None, :]
    tri_prev = (kk >= qq).astype(np.float32)
    tri_next = (kk <= qq).astype(np.float32)
    masks = np.concatenate([tri_prev, tri_next,
                            tri_prev if j > 0 else np.zeros_like(tri_prev),
                            tri_next if j < 3 else np.zeros_like(tri_next)], axis=1)
    return {
        "xh": xh, "ctx": np.ascontiguousarray(inp["ctx"][b]), "cT": cT,
        "ada_w0": inp["ada_w_0"], "ada_b0": inp["ada_b_0"].reshape(1, -1), "ng0": inp["norm_g_0"].reshape(1, -1),
        "w_in0": inp["a_w_in_0"], "w_o0": inp["a_w_o_0"], "sinks": inp["a_sinks_0"].reshape(1, -1),
        "cosA": cosA, "sinA": sinA, "masks": masks, "ident": np.eye(128, dtype=np.float32),
    }


def run_a(inp, trace=False):
    nc = build_prog_a()
    in_maps = [host_inputs_a(inp, c) for c in range(NCORE)]
    return run_bass_kernel_spmd(nc, in_maps, core_ids=list(range(NCORE)), trace=trace)


SCALE1 = 96.0 ** -0.5
NKEY = SEQ + CTX
NCH = 17


def emit_silu3(P, C, ps3, dst3, a, b):
    n = a * b
    u = C.t32.next()
    u3 = u[:, 0:n].rearrange("p (a b) -> p a b", a=a)
    P.act(u3, ps3, AF.Exp, scale=-1.0)
    P.act(u[:, 0:n], u[:, 0:n], AF.Ln, bias=C.one_t[:, 0:1])
    P.act(u[:, 0:n], u[:, 0:n], AF.Exp, scale=-1.0)
    P.tt("dve", dst3, ps3, u3, ALU.mult)


def alloc_l1_set(P, C):
    C.cqnT = P.sb([128, 2, TOK], BF16, "cqnT")
    C.sgT = P.sb([128, 8, TOK], BF16, "sgT1")
    C.ckvnT = P.sb([128, NKEY], BF16, "ckvnT")
    C.KT1 = [P.sb([96, 512], BF16, "KT1") for _ in range(NCH)]
    C.psMM = Ring([P.ps([128, 512], F32, "psMM1") for _ in range(2)])


def l1_block(P, C, x_view, A, B, cos_v, sin_v, do_q, ka_dst, kb_dst, kb_shift, tok_sl, hT=None):
    if hT is None:
        hb = C.hb.next()
        emit_modulate(P, C, x_view, A, B, hb)
        hT = C.hT.next()
        emit_hT(P, C, hb, hT)
    c0 = 0 if do_q else 256
    ps = C.psMM.next()
    emit_proj(P, hT, C.w_in1, c0, 416, ps)
    off = 256 - c0
    ks = C.ks.next()
    ss = C.ss.next()
    rstd = C.rstd.next()
    emit_rms_rstd(P, C, ps[:, off:off + 128], 128, ss, rstd)
    P.stt("dve", ks[:, 0:128], ps[:, off:off + 128], rstd[:, 0:1], C.kvg[:, :], ALU.mult, ALU.mult)
    if cos_v is not None:
        k32 = C.t32.next()
        P.copy("act", k32[:, 0:32], ps[:, off + 128:off + 160])
        emit_rope_tm(P, C, k32[:, 0:32], 1, 32, cos_v, sin_v, ks[:, 128:160])
    else:
        P.copy("act", ks[:, 128:160], ps[:, off + 128:off + 160])
    pk = C.psK
    P.transpose(pk[:, 0, :], ks[:, 0:128], C.ident[:, :])
    P.transpose(pk[0:32, 1, :], ks[:, 128:160], C.ident[:, :])
    P.copy("dve", ka_dst, pk[:, 0, :])
    P.copy("dve", kb_dst, pk[0:32, 1, :])
    if not do_q:
        return
    ss = C.ss.next()
    rstd = C.rstd.next()
    emit_rms_rstd(P, C, ps[:, 0:256], 256, ss, rstd)
    cqn = C.cqn.next()
    P.stt("dve", cqn[:, :], ps[:, 0:256], rstd[:, 0:1], C.qg[:, :], ALU.mult, ALU.mult)
    pt = C.psT
    for c in range(2):
        P.transpose(pt[:, c, :], cqn[:, c * 128:(c + 1) * 128], C.ident[:, :])
    P.copy("dve", C.cqnT[:, :, tok_sl], pt[:, 0:2, :])
    for p0 in (0, 4):
        pg = C.psMM.next()
        pg3 = pg[:, :].rearrange("p (a b) -> p a b", a=4)
        for p in range(4):
            col = 416 + (p0 + p) * 128
            for k in range(8):
                P.mm(pg3[:, p, :], C.w_in1[:, k, col:col + 128], hT[:, k, :], start=(k == 0), stop=(k == 7))
        emit_silu3(P, C, pg3, C.sgT[:, p0:p0 + 4, tok_sl], 4, 128)


def l1_gen_kv(P, C, h, c):
    ncol = 512 if c < 16 else 256
    nkb = ncol // 128
    base = c * 512
    ps = C.psG.next()
    P.mm(ps[0:64, 0:ncol], C.w_ukv[:, h * 128:h * 128 + 64], C.ckvnT[:, base:base + ncol])
    P.copy("dve", C.KT1[c][0:64, 0:ncol], ps[0:64, 0:ncol])
    ps2 = C.psG.next()
    for kb in range(nkb):
        P.mm(ps2[:, kb * 64:(kb + 1) * 64], C.ckvnT[:, base + kb * 128:base + (kb + 1) * 128],
             C.w_ukv[:, h * 128 + 64:h * 128 + 128])
    voff = 0 if h % 2 == 0 else 64
    P.copy("dve", C.VA1[c][:, 0:nkb, voff:voff + 64], ps2[:, 0:nkb * 64].rearrange("p (k d) -> p k d", k=nkb))
    P.memset("pool", C.VA1[c][:, 0:nkb, 64 - voff:128 - voff], 1.0)


def l1_gen_q(P, C, h, qt, QTh):
    sl = slice(qt * 512, (qt + 1) * 512)
    psA = C.psG.next()
    for c in range(2):
        P.mm(psA[0:96, :], C.w_uq[:, c, h * 96:(h + 1) * 96], C.cqnT[:, c, sl], start=(c == 0), stop=(c == 1))
    P.copy("dve", QTh[0:64, sl], psA[0:64, :])
    t = C.rt.next()
    P.tt("dve", t[64:96, :], psA[64:96, :], C.cosT[64:96, sl], ALU.mult)
    psB = C.psG.next()
    for c in range(2):
        P.mm(psB[0:96, :], C.w_uqs[:, c, h * 96:(h + 1) * 96], C.cqnT[:, c, sl], start=(c == 0), stop=(c == 1))
    u = C.rt.next()
    P.tt("dve", u[64:96, :], psB[64:96, :], C.sinT[64:96, sl], ALU.mult)
    P.tt("pool", QTh[64:96, sl], t[64:96, :], u[64:96, :], ALU.add)


def l1_attn_all(P, C, nheads, QTr):
    groups = []
    for h in range(nheads):
        for qt in range(4):
            gi = 0
            for c in range(NCH):
                nkb = 4 if c < 16 else 2
                for g0 in range(0, nkb, 2):
                    groups.append((h, qt, c, g0, gi, g0 + 2 >= nkb))
                    gi += 1
    qbuf = {}
    qbuf[0] = QTr.next()
    for c in range(NCH):
        l1_gen_kv(P, C, 0, c)
    for qt in range(4):
        l1_gen_q(P, C, 0, qt, qbuf[0])
    state = {}

    def front(grp):
        h, qt, c, g0, gi, last_of_chunk = grp
        sl = slice(qt * 512, (qt + 1) * 512)
        if gi == 0:
            state[(h, qt)] = C.psO.next()
        psS = C.psS.next()
        for kb in range(2):
            P.mm(psS[:, kb, :], C.KT1[c][0:96, (g0 + kb) * 128:(g0 + kb + 1) * 128], qbuf[h][0:96, sl])
        pt = C.pT1.next()
        P.act(pt[:, :, :], psS[:, :, :], AF.Exp, scale=SCALE1)
        return pt

    def back(grp, pt):
        h, qt, c, g0, gi, last_of_chunk = grp
        sl = slice(qt * 512, (qt + 1) * 512)
        psO = state[(h, qt)]
        for kb in range(2):
            P.mm(psO[:, :], C.VA1[c][:, g0 + kb, :], pt[:, kb, :],
                 start=(gi == 0 and kb == 0), stop=(gi == 32 and kb == 1))
        if last_of_chunk and qt == 3 and h + 1 < nheads:
            if c == 0:
                qbuf[h + 1] = QTr.next()
            l1_gen_kv(P, C, h + 1, c)
            if c < 4:
                l1_gen_q(P, C, h + 1, c, qbuf[h + 1])
        if gi == 32:
            p = h // 2
            if h % 2 == 0:
                lo, hi, slo, shi = 0, 64, 64, 128
            else:
                lo, hi, slo, shi = 64, 128, 0, 64
            rec = C.rec1.next()
            P.recip(rec[lo:hi, :], psO[slo:shi, :])
            o = C.rec1.next()
            P.tt("dve", o[lo:hi, :], psO[lo:hi, :], rec[lo:hi, :], ALU.mult)
            P.tt("pool", C.sgT[lo:hi, p, sl], o[lo:hi, :], C.sgT[lo:hi, p, sl], ALU.mult)

    prev = None
    for grp in groups:
        pt = front(grp)
        if prev is not None:
            back(*prev)
        prev = (grp, pt)
    if prev is not None:
        back(*prev)


def build_layer1(P, C, io):
    fused = io.mode == "fused"
    with Phase(P):
        alloc_l1_set(P, C)
        if not fused:
            P.dma("pool", C.ident[:, :], io.ident)
            P.memset("pool", C.ones32[:, :], 1.0)

        with Phase(P):
            C.w_in1 = P.sb([128, 8, 1440], BF16, "w_in1")
            C.psT = P.ps([128, 8, 128], BF16, "psT1")
            C.psK = P.ps([128, 2, 128], BF16, "psK1")
            C.hb = Ring([P.sb([128, D], BF16, "hb1") for _ in range(2)])
            C.hT = Ring([P.sb([128, 8, 128], BF16, "hT1") for _ in range(3)])
            C.ks = Ring([P.sb([128, 160], BF16, "ks") for _ in range(2)])
            C.cqn = Ring([P.sb([128, 256], BF16, "cqn") for _ in range(2)])
            C.qg = P.sb([128, 256], F32, "qg")
            C.kvg = P.sb([128, 128], F32, "kvg")
            cosB = P.sb([128, NB, 32], F32, "cosB")
            sinB = P.sb([128, NB, 32], F32, "sinB")
            kst_a = P.sb([128, TOK], BF16, "kst_a")
            kst_b = P.sb([32, TOK], BF16, "kst_b")
            xin = Ring([P.sb([128, D], F32, "xin1") for _ in range(3)])
            A1c = P.sb([128, D], F32, "A1c")
            B1c = P.sb([128, D], F32, "B1c")
            wv = io.w_in1.rearrange("(k p) n -> p k n", p=128)
            for k in range(8):
                P.dma("pool", C.w_in1[:, k, :], wv[:, k, :])
            P.dma("sp", C.qg[:, :], io.qg.partition_broadcast(128))
            P.dma("sp", C.kvg[:, :], io.kvg.partition_broadcast(128))
            P.dma("sp", cosB[:, :, :], io.cosB.rearrange("(n p) d -> p n d", p=128))
            P.dma("sp", sinB[:, :, :], io.sinB.rearrange("(n p) d -> p n d", p=128))
            with Phase(P):
                alloc_ada_tmp(P, C)
                emit_silu_c(P, C, io.cT)
                emit_ada(P, C, io.ada_w1, io.ada_b1, io.ng1, (C.B_b, C.A_b, C.G_b), (B1c, A1c, None))
            stop = getattr(io, "stop_after", None)
            if stop == "pre0":
                return
            ctx1v = io.ctx1_src.rearrange("(n p) d -> p n d", p=128)
            for cb in range(2):
                xt = xin.next()
                P.dma("sp", xt[:, :], ctx1v[:, cb, :])
                kb_tmp = C.kbt
                l1_block(P, C, xt[:, :], A1c, B1c, None, None, False,
                         C.ckvnT[:, SEQ + cb * 128:SEQ + (cb + 1) * 128], kb_tmp[0:32, :], None, None)
                P.copy("dve", C.KT1[16][64:96, cb * 128:(cb + 1) * 128], kb_tmp[0:32, :])
            if stop == "pre1":
                return
            x1v = io.x1_own.rearrange("(n p) d -> p n d", p=128)
            hTs = {}
            for i in range(NB + 1):
                if i < NB:
                    xt = xin.next()
                    P.dma("sp", xt[:, :], x1v[:, i, :])
                    hb = C.hb.next()
                    emit_modulate(P, C, xt[:, :], C.A_b, C.B_b, hb)
                    hTs[i] = C.hT.next()
                    emit_hT(P, C, hb, hTs[i])
                if i >= 1:
                    k = i - 1
                    tsl = slice(k * 128, (k + 1) * 128)
                    l1_block(P, C, None, None, None, cosB[:, k, :], sinB[:, k, :], True,
                             kst_a[:, tsl], kst_b[0:32, tsl], None, tsl, hT=hTs[k])
            if stop == "pre2":
                return
            gath = io.gath
            if fused:
                bnc = io.bounce
                P.dma("pool", bnc[0:128, :], kst_a[:, :])
                P.dma("pool", bnc[128:160, :], kst_b[0:32, :])
                P.all_gather(gath.tile, bnc.tile, [[0, 1, 2, 3], [4, 5, 6, 7]])
            else:
                cosF = P.sb([128, 4 * NB, 32], F32, "cosF")
                sinF = P.sb([128, 4 * NB, 32], F32, "sinF")
                cfv = io.cosF.rearrange("(n p) d -> p n d", p=128)
                sfv = io.sinF.rearrange("(n p) d -> p n d", p=128)
                for r in range(4):
                    P.dma("sp", cosF[:, r * NB:(r + 1) * NB, :], cfv[:, r * NB:(r + 1) * NB, :])
                    P.dma("sp", sinF[:, r * NB:(r + 1) * NB, :], sfv[:, r * NB:(r + 1) * NB, :])
                xfv = io.x1_full.rearrange("(n p) d -> p n d", p=128)
                for r in range(4):
                    for i in range(NB):
                        xt = xin.next()
                        P.dma("sp", xt[:, :], xfv[:, r * NB + i, :])
                        tsl = slice(i * 128, (i + 1) * 128)
                        l1_block(P, C, xt[:, :], C.A_b, C.B_b, cosF[:, r * NB + i, :], sinF[:, r * NB + i, :], False,
                                 kst_a[:, tsl], kst_b[0:32, tsl], None, None)
                    P.dma("pool", gath[r * 160:r * 160 + 128, :], kst_a[:, :])
                    P.dma("pool", gath[r * 160 + 128:r * 160 + 160, :], kst_b[0:32, :])
            if stop == "pre3":
                return
            for r in range(4):
                P.dma("sp", C.ckvnT[:, r * TOK:(r + 1) * TOK], gath[r * 160:r * 160 + 128, :])
                for cc in range(4):
                    P.dma("sp", C.KT1[r * 4 + cc][64:96, :], gath[r * 160 + 128:r * 160 + 160, cc * 512:(cc + 1) * 512])

        if getattr(io, "stop_after", None) == "pre":
            return
        with Phase(P):
            C.VA1 = [P.sb([128, 4, 128], BF16, "VA1") for _ in range(NCH)]
            C.psS = Ring([P.ps([128, 2, 512], F32, "psS1") for _ in range(2)])
            C.psO = Ring([P.ps([128, 512], F32, "psO1") for _ in range(2)])
            QTr = Ring([P.sb([96, TOK], BF16, "QTh") for _ in range(2)])
            C.cosT = P.sb([96, TOK], F32, "cosT")
            C.sinT = P.sb([96, TOK], F32, "sinT")
            C.pT1 = Ring([P.sb([128, 2, 512], BF16, "pT1") for _ in range(3)])
            C.w_uq = P.sb([128, 2, 1536], BF16, "w_uq")
            C.w_uqs = P.sb([128, 2, 1536], BF16, "w_uqs")
            C.w_ukv = P.sb([128, 2048], BF16, "w_ukv")
            C.rec1 = Ring([P.sb([128, 512], F32, "rec1") for _ in range(4)])
            C.rt = Ring([P.sb([96, 512], F32, "rt") for _ in range(4)])
            C.psG = C.psMM
            P.dma("pool", C.w_uq[:, :, :], io.w_uq.rearrange("(c p) n -> p c n", p=128))
            P.dma("pool", C.w_uqs[:, :, :], io.w_uqs.rearrange("(c p) n -> p c n", p=128))
            P.dma("pool", C.w_ukv[:, :], io.w_ukv)
            P.dma("sp", C.cosT[64:96, :], io.cosBT)
            P.dma("sp", C.sinT[64:96, :], io.sinBT)
            l1_attn_all(P, C, io.nheads, QTr)

        if getattr(io, "stop_after", None) in ("genkv", "gen"):
            return
        with Phase(P):
            C.w_o1 = P.sb([128, 8, 1024], BF16, "w_o1")
            C.fg = P.sb([128, D], F32, "fg")
            wov = io.w_o1.rearrange("(k p) n -> p k n", p=128)
            for k0 in range(0, 8, 4):
                P.dma("pool", C.w_o1[:, k0:k0 + 4, :], wov[:, k0:k0 + 4, :])
            P.dma("sp", C.fg[:, :], io.fg.partition_broadcast(128))
            xin = Ring([P.sb([128, D], F32, "xin2") for _ in range(3)])
            xo = Ring([P.sb([128, D], F32, "xo") for _ in range(2)])
            yo = Ring([P.sb([128, D], F32, "yo") for _ in range(2)])
            x1v = io.x1_own.rearrange("(n p) d -> p n d", p=128)
            outv = io.out.rearrange("(n p) d -> p n d", p=128)
            xs = {}

            def load(i):
                xs[i] = xin.next()
                P.dma("sp", xs[i][:, :], x1v[:, i, :])

            load(0)
            for tb in range(NB):
                if tb + 1 < NB:
                    load(tb + 1)
                xt = xs[tb]
                xot = xo.next()
                for n in range(2):
                    ps = C.psMM.next()
                    for p in range(8):
                        P.mm(ps[:, :], C.sgT[:, p, tb * 128:(tb + 1) * 128], C.w_o1[:, p, n * 512:(n + 1) * 512],
                             start=(p == 0), stop=(p == 7))
                    t = C.t32.next()
                    sl = slice(n * 512, (n + 1) * 512)
                    P.tt("dve", t[:, 0:512], ps[:, :], C.G_b[:, sl], ALU.mult)
                    P.tt("pool", xot[:, sl], t[:, 0:512], xt[:, sl], ALU.add)
                ss = C.ss.next()
                rstd = C.rstd.next()
                emit_rms_rstd(P, C, xot[:, :], D, ss, rstd)
                yt = yo.next()
                P.stt("dve", yt[:, :], xot[:, :], rstd[:, 0:1], C.fg[:, :], ALU.mult, ALU.mult)
                P.dma("pool", outv[:, tb, :], yt[:, :], final=True)


def declare_inputs_b(nc, io):
    def din(name, shape, dt=F32):
        return nc.dram_tensor(name, list(shape), dt, kind="ExternalInput").ap()

    io.ada_w1 = din("ada_w1", [D, 3 * D])
    io.ada_b1 = din("ada_b1", [1, 3 * D])
    io.ng1 = din("ng1", [1, D])
    io.w_in1 = din("w_in1", [D, 1440])
    io.qg = din("qg", [1, 256])
    io.kvg = din("kvg", [1, 128])
    io.w_uq = din("w_uq", [256, 1536])
    io.w_uqs = din("w_uqs", [256, 1536])
    io.w_ukv = din("w_ukv", [128, 2048])
    io.w_o1 = din("w_o1", [D, D])
    io.fg = din("fg", [1, D])
    io.cosB = din("cosB", [TOK, 32])
    io.sinB = din("sinB", [TOK, 32])
    io.cosBT = din("cosBT", [32, TOK])
    io.sinBT = din("sinBT", [32, TOK])


def host_inputs_b(inp, c):
    b, j = c // 4, c % 4
    pos = np.arange(TOK * j, TOK * (j + 1))
    cosB, sinB = rope_tables(pos, 32)
    w_uq = inp["b_w_uq_1"]
    w3 = w_uq.reshape(256, 16, 96)
    sw = w3.copy()
    sw[:, :, 64:72], sw[:, :, 72:80] = w3[:, :, 72:80], w3[:, :, 64:72]
    sw[:, :, 80:88], sw[:, :, 88:96] = w3[:, :, 88:96], w3[:, :, 80:88]
    return {
        "ada_w1": inp["ada_w_1"], "ada_b1": inp["ada_b_1"].reshape(1, -1), "ng1": inp["norm_g_1"].reshape(1, -1),
        "w_in1": inp["b_w_in_1"], "qg": inp["b_q_norm_1"].reshape(1, -1), "kvg": inp["b_kv_norm_1"].reshape(1, -1),
        "w_uq": w_uq, "w_uqs": np.ascontiguousarray(sw.reshape(256, 1536)), "w_ukv": inp["b_w_ukv_1"],
        "w_o1": inp["b_w_o_1"], "fg": inp["final_g"].reshape(1, -1),
        "cosB": cosB, "sinB": sinB, "cosBT": np.ascontiguousarray(cosB.T), "sinBT": np.ascontiguousarray(sinB.T),
    }


NHEADS_DBG = 16
STOP_AFTER = None


def build_prog_b():
    nc = bass.Bass("TRN2", target_bir_lowering=False)
    io = Ctx()
    io.mode = "unfused"
    io.nheads = NHEADS_DBG
    io.stop_after = STOP_AFTER

    def din(name, shape, dt=F32):
        return nc.dram_tensor(name, list(shape), dt, kind="ExternalInput").ap()

    declare_inputs_b(nc, io)
    io.cT = din("cT", [128, 16])
    io.ident = din("ident", [128, 128])
    io.x1_own = din("x1_own", [TOK, D])
    io.x1_full = din("x1_full", [SEQ, D])
    io.ctx1_src = din("ctx1", [CTX, D])
    io.cosF = din("cosF", [SEQ, 32])
    io.sinF = din("sinF", [SEQ, 32])
    io.out = nc.dram_tensor("out", [TOK, D], F32, kind="ExternalOutput").ap()
    gath_h = nc.dram_tensor("gath", [640, TOK], BF16)
    with ExitStack() as st:
        P = Prog(nc, st)
        C = Ctx()
        alloc_globals(P, C)
        C.kbt = P.sb([32, 128], BF16, "kbt")
        gt = Tile(gath_h, "gath")
        io.gath = gt[:, :]
        build_layer1(P, C, io)
        P.flush(last=True)
        print("prog B: ninst", P.ninst, "nsem", P.nsem)
    return nc


def run_b(inp, x1, ctx1, trace=False):
    nc = build_prog_b()
    cosF, sinF = rope_tables(np.arange(SEQ), 32)
    in_maps = []
    for c in range(NCORE):
        b, j = c // 4, c % 4
        m = host_inputs_b(inp, c)
        a = host_inputs_a(inp, c)
        m["cT"] = a["cT"]
        m["ident"] = a["ident"]
        m["x1_own"] = np.ascontiguousarray(x1[b, j * TOK:(j + 1) * TOK])
        m["x1_full"] = np.ascontiguousarray(x1[b])
        m["ctx1"] = np.ascontiguousarray(ctx1[b])
        m["cosF"] = cosF
        m["sinF"] = sinF
        in_maps.append(m)
    return run_bass_kernel_spmd(nc, in_maps, core_ids=list(range(NCORE)), trace=trace)


def build_prog_fused():
    nc = bass.Bass("TRN2", target_bir_lowering=False)
    io = Ctx()
    declare_inputs_a(nc, io)
    declare_inputs_b(nc, io)
    io.out = nc.dram_tensor("out", [TOK, D], F32, kind="ExternalOutput").ap()
    x1s = nc.dram_tensor("x1s", [TOK, D], F32)
    ctx1s = nc.dram_tensor("ctx1s", [CTX, D], F32)
    bounce = nc.dram_tensor("bounce", [160, TOK], BF16)
    gath = nc.dram_tensor("gath", [640, TOK], BF16)
    with ExitStack() as st:
        P = Prog(nc, st)
        C = Ctx()
        alloc_globals(P, C)
        C.kbt = P.sb([32, 128], BF16, "kbt")
        x1t = Tile(x1s, "x1s")
        ctx1t = Tile(ctx1s, "ctx1s")
        io.x1 = x1t[:, :]
        io.ctx1 = ctx1t[:, :]
        io.final_x1 = False
        io.on_x1 = None
        io.on_ctx1 = None
        build_layer0(P, C, io)
        io.mode = "fused"
        io.nheads = NHEADS_DBG
        io.stop_after = STOP_AFTER
        io.x1_own = x1t[:, :]
        io.ctx1_src = ctx1t[:, :]
        io.gath = Tile(gath, "gath")[:, :]
        io.bounce = Tile(bounce, "bounce")[:, :]
        build_layer1(P, C, io)
        P.flush(last=True)
        print("prog fused: ninst", P.ninst, "nsem", P.nsem)
    return nc


def host_inputs_fused(inp, c):
    m = host_inputs_a(inp, c)
    m.update(host_inputs_b(inp, c))
    return m


def run_fused(inp, trace=False):
    nc = build_prog_fused()
    in_maps = [host_inputs_fused(inp, c) for c in range(NCORE)]
    return run_bass_kernel_spmd(nc, in_maps, core_ids=list(range(NCORE)), trace=trace)


FUSED = True


def kernel(**inputs):
    inp = {k: np.asarray(v) for k, v in inputs.items()}
    out = np.zeros((2, SEQ, D), np.float32)
    if FUSED:
        res = run_fused(inp)
    else:
        res = run_a(inp)
        x1 = np.zeros((2, SEQ, D), np.float32)
        ctx1 = np.zeros((2, CTX, D), np.float32)
        for c in range(NCORE):
            b, j = c // 4, c % 4
            x1[b, j * TOK:(j + 1) * TOK] = res.results[c]["x1"]
            ctx1[b] = res.results[c]["ctx1"]
        res = run_b(inp, x1, ctx1)
    for c in range(NCORE):
        b, j = c // 4, c % 4
        out[b, j * TOK:(j + 1) * TOK] = res.results[c]["out"]
    return out
```
